# Optimizing a Trainium2 kernel written in Bass

```python
import math
import jax
import jax.numpy as jnp
from jax import lax
import numpy as np

D_MODEL = 2048
BATCH = 4
SEQ = 4096
DEPTH = 1

HEAD_DIM = 64
N_Q_HEADS = 16
N_KV_HEADS = 4
GQA_GROUP = N_Q_HEADS // N_KV_HEADS
ATTN_WIDTH = N_Q_HEADS * HEAD_DIM
KV_WIDTH = N_KV_HEADS * HEAD_DIM
WINDOW = 128
BLOCK = 128
NUM_BUCKETS = 32
MAX_DISTANCE = 128
NEG_INF = -1e30
SSM_WIDTH = D_MODEL // 4
SSM_GROUP_CH = 16
SSM_GROUPS = SSM_WIDTH // SSM_GROUP_CH
SSM_STATE = 64
D_FF = 4 * D_MODEL
N_BRANCHES = 2
IN_WIDTH = ATTN_WIDTH + 2 * KV_WIDTH + SSM_WIDTH + N_BRANCHES * D_MODEL
N_MOD = 6
EPS = 1e-6

kernel_name = "hybrid_swa_sink_s5_gated_adaln_block"


def _rmsnorm(x, g):
    xf = x.astype(jnp.float32)
    y = xf * lax.rsqrt(jnp.mean(xf * xf, axis=-1, keepdims=True) + EPS)
    return (y * g.astype(jnp.float32)).astype(x.dtype)


def _modulate(h, shift, scale):
    return h * (1 + scale[:, None, :]) + shift[:, None, :]


def _t5_buckets_block():
    qi = np.arange(BLOCK)[:, None]
    ki = np.arange(2 * BLOCK)[None, :]
    n = np.maximum(qi + BLOCK - ki, 0)
    max_exact = NUM_BUCKETS // 2
    large = max_exact + (np.log(np.maximum(n, 1) / max_exact)
                         / np.log(MAX_DISTANCE / max_exact)
                         * (NUM_BUCKETS - max_exact)).astype(np.int32)
    large = np.minimum(large, NUM_BUCKETS - 1)
    return np.where(n < max_exact, n, large).astype(np.int32)


def _sliding_window_attention(q, k, v, sinks, bias):
    b, s, _ = q.shape
    nb = s // BLOCK
    q = q.reshape(b, nb, BLOCK, N_KV_HEADS, GQA_GROUP, HEAD_DIM)
    pad = ((0, 0), (BLOCK, 0), (0, 0))
    kp = jnp.pad(k, pad).reshape(b, nb + 1, BLOCK, N_KV_HEADS, HEAD_DIM)
    vp = jnp.pad(v, pad).reshape(b, nb + 1, BLOCK, N_KV_HEADS, HEAD_DIM)
    kk = jnp.concatenate([kp[:, :-1], kp[:, 1:]], axis=2)
    vv = jnp.concatenate([vp[:, :-1], vp[:, 1:]], axis=2)
    scores = jnp.einsum('bnqhgd,bnkhd->bnhgqk', q, kk).astype(jnp.float32)
    scores = scores * (HEAD_DIM ** -0.5) + bias
    qi = jnp.arange(BLOCK)[:, None]
    ki = jnp.arange(2 * BLOCK)[None, :]
    dist = qi + BLOCK - ki
    band = (dist >= 0) & (dist < WINDOW)
    blk = jnp.arange(nb)[:, None, None]
    valid = band[None] & (blk * BLOCK + ki[None] - BLOCK >= 0)
    scores = jnp.where(valid[None, :, None, None], scores, NEG_INF)
    sink = sinks.astype(jnp.float32).reshape(N_KV_HEADS, GQA_GROUP, 1)
    m = jnp.maximum(jnp.max(scores, axis=-1), sink)
    p = jnp.exp(scores - m[..., None])
    denom = jnp.sum(p, axis=-1) + jnp.exp(sink - m)
    p = (p / denom[..., None]).astype(vv.dtype)
    o = jnp.einsum('bnhgqk,bnkhd->bnqhgd', p, vv)
    return o.reshape(b, s, ATTN_WIDTH)


def _ssm_combine(e1, e2):
    (a1r, a1i), (b1r, b1i) = e1
    (a2r, a2i), (b2r, b2i) = e2
    a_new = (a1r * a2r - a1i * a2i, a1r * a2i + a1i * a2r)
    b_new = (a2r * b1r - a2i * b1i + b2r, a2r * b1i + a2i * b1r + b2i)
    return (a_new, b_new)


def _s5_ssm(u, lambda_re, lambda_im, log_step, b_re, b_im, c_re, c_im, d_skip):
    bsz, s, _ = u.shape
    f32 = jnp.float32
    uf = u.astype(f32).reshape(bsz, s, SSM_GROUPS, SSM_GROUP_CH)
    lam_re = jnp.minimum(lambda_re.astype(f32), -1e-4)
    lam_im = lambda_im.astype(f32)
    delta = jnp.exp(log_step.astype(f32))[:, None]
    mag = jnp.exp(lam_re * delta)
    ang = lam_im * delta
    abar_re, abar_im = mag * jnp.cos(ang), mag * jnp.sin(ang)
    num_re, num_im = abar_re - 1.0, abar_im
    den = lam_re * lam_re + lam_im * lam_im
    f_re = (num_re * lam_re + num_im * lam_im) / den
    f_im = (num_im * lam_re - num_re * lam_im) / den
    br, bi = b_re.astype(f32), b_im.astype(f32)
    bbar_re = f_re[..., None] * br - f_im[..., None] * bi
    bbar_im = f_re[..., None] * bi + f_im[..., None] * br
    bu_re = jnp.einsum('bsgp,gnp->bsgn', uf, bbar_re)
    bu_im = jnp.einsum('bsgp,gnp->bsgn', uf, bbar_im)
    shape_a = (1, s, SSM_GROUPS, SSM_STATE)
    a_re = jnp.broadcast_to(abar_re, shape_a)
    a_im = jnp.broadcast_to(abar_im, shape_a)
    _, (x_re, x_im) = lax.associative_scan(
        _ssm_combine, ((a_re, a_im), (bu_re, bu_im)), axis=1)
    y = (jnp.einsum('bsgn,gpn->bsgp', x_re, c_re.astype(f32))
         - jnp.einsum('bsgn,gpn->bsgp', x_im, c_im.astype(f32))
         + d_skip.astype(f32).reshape(SSM_GROUPS, SSM_GROUP_CH) * uf)
    return y.reshape(bsz, s, SSM_WIDTH).astype(u.dtype)


def setup_inputs(seed: int = 0) -> dict:
    key = jax.random.key(seed)
    ks = jax.random.split(key, 32)

    def nrm(k, shape, scale):
        return jax.random.normal(k, shape, jnp.float32) * scale

    G, N, P = SSM_GROUPS, SSM_STATE, SSM_GROUP_CH
    lam_im0 = jnp.pi * jnp.arange(N, dtype=jnp.float32)
    return {
        "x": nrm(ks[0], (BATCH, SEQ, D_MODEL), 1.0),
        "c": nrm(ks[1], (BATCH, D_MODEL), 1.0),
        "w_ada": nrm(ks[2], (DEPTH, D_MODEL, N_MOD * D_MODEL), 0.5 * D_MODEL ** -0.5),
        "b_ada": nrm(ks[3], (DEPTH, N_MOD * D_MODEL), 0.02),
        "norm1_g": 1.0 + nrm(ks[4], (DEPTH, D_MODEL), 0.02),
        "w_in": nrm(ks[5], (DEPTH, D_MODEL, IN_WIDTH), D_MODEL ** -0.5),
        "b_in": nrm(ks[6], (DEPTH, IN_WIDTH), 0.02),
        "attn_sinks": nrm(ks[7], (DEPTH, N_Q_HEADS), 0.5),
        "rel_bias": nrm(ks[8], (NUM_BUCKETS, N_Q_HEADS), 0.1),
        "lambda_re": -0.5 + nrm(ks[9], (DEPTH, G, N), 0.01),
        "lambda_im": lam_im0 + nrm(ks[10], (DEPTH, G, N), 0.01),
        "log_step": jax.random.uniform(ks[11], (DEPTH, G), jnp.float32,
                                       minval=math.log(1e-3), maxval=math.log(1e-1)),
        "ssm_b_re": nrm(ks[12], (DEPTH, G, N, P), (2 * P) ** -0.5),
        "ssm_b_im": nrm(ks[13], (DEPTH, G, N, P), (2 * P) ** -0.5),
        "ssm_c_re": nrm(ks[14], (DEPTH, G, P, N), (2 * N) ** -0.5),
        "ssm_c_im": nrm(ks[15], (DEPTH, G, P, N), (2 * N) ** -0.5),
        "ssm_d": nrm(ks[16], (DEPTH, SSM_WIDTH), 1.0),
        "w_glu": nrm(ks[17], (DEPTH, SSM_WIDTH, SSM_WIDTH), SSM_WIDTH ** -0.5),
        "b_glu": nrm(ks[18], (DEPTH, SSM_WIDTH), 0.02),
        "w_attn_proj": nrm(ks[19], (DEPTH, ATTN_WIDTH, D_MODEL), ATTN_WIDTH ** -0.5),
        "w_ssm_proj": nrm(ks[20], (DEPTH, SSM_WIDTH, D_MODEL), SSM_WIDTH ** -0.5),
        "w_out": nrm(ks[21], (DEPTH, D_MODEL, D_MODEL), D_MODEL ** -0.5),
        "norm2_g": 1.0 + nrm(ks[22], (DEPTH, D_MODEL), 0.02),
        "w_ff1": nrm(ks[23], (DEPTH, D_MODEL, D_FF), D_MODEL ** -0.5),
        "w_ff2": nrm(ks[24], (DEPTH, D_FF, D_MODEL), D_FF ** -0.5),
        "final_g": 1.0 + nrm(ks[25], (D_MODEL,), 0.02),
    }


def reference(x, c, w_ada, b_ada, norm1_g, w_in, b_in, attn_sinks, rel_bias,
              lambda_re, lambda_im, log_step, ssm_b_re, ssm_b_im, ssm_c_re,
              ssm_c_im, ssm_d, w_glu, b_glu, w_attn_proj, w_ssm_proj, w_out,
              norm2_g, w_ff1, w_ff2, final_g):
    bsz, s, _ = x.shape
    buckets = jnp.asarray(_t5_buckets_block())
    bias = rel_bias.astype(jnp.float32)[buckets]
    bias = jnp.transpose(bias, (2, 0, 1)).reshape(N_KV_HEADS, GQA_GROUP, BLOCK, 2 * BLOCK)
    splits = [ATTN_WIDTH, ATTN_WIDTH + KV_WIDTH, ATTN_WIDTH + 2 * KV_WIDTH,
              ATTN_WIDTH + 2 * KV_WIDTH + SSM_WIDTH,
              ATTN_WIDTH + 2 * KV_WIDTH + SSM_WIDTH + D_MODEL]
    cs = jax.nn.silu(c)
    for l in range(DEPTH):
        mod = cs @ w_ada[l] + b_ada[l]
        sh1, sc1, g1, sh2, sc2, g2 = jnp.split(mod, N_MOD, axis=-1)
        h = _modulate(_rmsnorm(x, norm1_g[l]), sh1, sc1)
        proj = h @ w_in[l] + b_in[l]
        q, k, v, u, gate_a, gate_s = jnp.split(proj, splits, axis=-1)
        attn = _sliding_window_attention(q, k, v, attn_sinks[l], bias)
        y_attn = attn @ w_attn_proj[l]
        y = _s5_ssm(u, lambda_re[l], lambda_im[l], log_step[l], ssm_b_re[l],
                    ssm_b_im[l], ssm_c_re[l], ssm_c_im[l], ssm_d[l])
        z = jax.nn.gelu(y)
        z = z * jax.nn.sigmoid(z @ w_glu[l] + b_glu[l])
        y_ssm = z @ w_ssm_proj[l]
        merged = jax.nn.sigmoid(gate_a) * y_attn + jax.nn.sigmoid(gate_s) * y_ssm
        x = x + g1[:, None, :] * (merged @ w_out[l])
        h2 = _modulate(_rmsnorm(x, norm2_g[l]), sh2, sc2)
        ff = jnp.square(jax.nn.relu(h2 @ w_ff1[l])) @ w_ff2[l]
        x = x + g2[:, None, :] * ff
    return _rmsnorm(x, final_g)
```

```python
import contextlib
import math
import numpy as np
import concourse.bass as bass
import concourse.mybir as mybir
from concourse.bass_utils import run_bass_kernel_spmd

F32 = mybir.dt.float32
BF16 = mybir.dt.bfloat16
I32 = mybir.dt.int32
AF = mybir.ActivationFunctionType
ALU = mybir.AluOpType
AX = mybir.AxisListType

ENGS = ("pe", "act", "dve", "pool", "sp")
NCORES = 8
TOK = 2048
ST = 1024
D = 2048
NEG = -30000.0
BARRIERS = False


class Op:
    __slots__ = ("eng", "fn", "deps", "needs_inc", "count", "is_dma", "dsem", "dval")

    def __init__(self, eng, fn, is_dma):
        self.eng = eng
        self.fn = fn
        self.deps = []
        self.needs_inc = False
        self.count = 0
        self.is_dma = is_dma
        self.dsem = None
        self.dval = 0


class Sched:
    NDSEM = 8

    def __init__(self, nc):
        self.nc = nc
        self.ops = {e: [] for e in ENGS}
        self.lastw = {}
        self.readers = {}
        self.fence = {}
        self.gfence = []
        self.dma_rr = {e: 0 for e in ENGS}
        self.dma_cnt = {}
        self.dma_last = {}
        self.final = []

    @staticmethod
    def _buf(key):
        return key[0] if isinstance(key, tuple) else key

    @staticmethod
    def _is_psum(k):
        b = k[0] if isinstance(k, tuple) else k
        return isinstance(b, str) and len(b) == 3 and b.startswith("pb")

    def add(self, eng, fn, reads=(), writes=(), dma=False):
        pkeys = []
        for k in list(reads) + list(writes):
            if self._is_psum(k):
                kk = ((k[0] if isinstance(k, tuple) else k), "x")
                if kk not in pkeys:
                    pkeys.append(kk)
        reads = [k for k in reads if not self._is_psum(k)]
        writes = [k for k in writes if not self._is_psum(k)] + pkeys
        op = Op(eng, fn, dma)
        deps = list(self.gfence)
        for k in reads:
            w = self.lastw.get(k)
            if w is not None:
                deps.append(w)
            deps.extend(self.fence.get(self._buf(k), ()))
        for k in writes:
            w = self.lastw.get(k)
            if w is not None:
                deps.append(w)
            deps.extend(self.readers.get(k, ()))
            deps.extend(self.fence.get(self._buf(k), ()))
        if dma:
            slot = (eng, self.dma_rr[eng] % self.NDSEM)
            self.dma_rr[eng] += 1
            self.dma_cnt[slot] = self.dma_cnt.get(slot, 0) + 1
            op.dsem = slot
            op.dval = 16 * self.dma_cnt[slot]
            prev = self.dma_last.get(slot)
            if prev is not None:
                deps.append(prev)
            self.dma_last[slot] = op
        seen = set()
        for d in deps:
            if id(d) in seen:
                continue
            seen.add(id(d))
            if (not d.is_dma) and d.eng == eng and eng == "pe":
                continue
            if not d.is_dma:
                d.needs_inc = True
            op.deps.append(d)
        for k in reads:
            self.readers.setdefault(k, []).append(op)
        for k in writes:
            self.lastw[k] = op
            self.readers[k] = []
        self.ops[eng].append(op)
        return op

    def collect(self, buf):
        ops = list(self.fence.pop(buf, ()))
        for k in list(self.lastw.keys()):
            if self._buf(k) == buf:
                ops.append(self.lastw.pop(k))
        for k in list(self.readers.keys()):
            if self._buf(k) == buf:
                ops.extend(self.readers.pop(k))
        uniq = {}
        for o in ops:
            uniq[id(o)] = o
        return list(uniq.values())

    def retire(self, buf):
        self.fence[buf] = self.collect(buf)

    def barrier(self):
        ops = []
        for e in ENGS:
            for o in reversed(self.ops[e]):
                if not o.is_dma:
                    ops.append(o)
                    break
        ops.extend(self.dma_last.values())
        self.gfence = ops

    def finish(self, eng, ops):
        self.final.append((eng, list(ops)))

    def emit(self):
        nc = self.nc
        for eng, ops in self.final:
            for o in ops:
                if not o.is_dma:
                    o.needs_inc = True
        for e in ENGS:
            c = 0
            for op in self.ops[e]:
                if op.needs_inc and not op.is_dma:
                    c += 1
                    op.count = c
        with contextlib.ExitStack() as st:
            esem = {e: st.enter_context(nc.semaphore("s_" + e)) for e in ENGS}
            dsem = {}
            for slot in self.dma_cnt:
                dsem[slot] = st.enter_context(nc.semaphore("d_%s_%d" % slot))
            block = st.enter_context(nc.Block())

            def comp(d):
                if d.is_dma:
                    return dsem[d.dsem], d.dval
                return esem[d.eng], d.count

            def run(e, engine):
                known = {}

                def waits(deps):
                    for d in deps:
                        s, v = comp(d)
                        if known.get(id(s), 0) < v:
                            engine.wait_ge(s, v)
                            known[id(s)] = v

                for op in self.ops[e]:
                    waits(op.deps)
                    ins = op.fn(engine)
                    if op.is_dma:
                        ins.then_inc(dsem[op.dsem], 16)
                    elif op.needs_inc:
                        ins.then_inc(esem[e], 1)
                for eng, ops in self.final:
                    if eng == e:
                        waits(ops)

            block.tensor(lambda t: run("pe", t))
            block.scalar(lambda t: run("act", t))
            block.vector(lambda t: run("dve", t))
            block.gpsimd(lambda t: run("pool", t))
            block.sync(lambda t: run("sp", t))


class Arena:
    LO, HI = 16640, 229120

    def __init__(self, nc, sched):
        self.nc = nc
        self.s = sched
        self.free = [(self.LO, self.HI)]
        self.live = {}
        self.dead = []
        self.n = 0
        self.peak = 0

    def alloc(self, name, shape, dtype, top=False):
        nbytes = int(np.prod(shape[1:])) * mybir.dt.size(dtype)
        nbytes = (nbytes + 63) // 64 * 64
        order = list(enumerate(self.free))
        if top:
            order = order[::-1]
        for i, (lo, hi) in order:
            if hi - lo >= nbytes:
                if top:
                    off = hi - nbytes
                    if hi - lo == nbytes:
                        self.free.pop(i)
                    else:
                        self.free[i] = (lo, hi - nbytes)
                else:
                    off = lo
                    if hi - lo == nbytes:
                        self.free.pop(i)
                    else:
                        self.free[i] = (lo + nbytes, hi)
                break
        else:
            raise MemoryError("SBUF arena full allocating %s (%d B); free=%s" % (name, nbytes, self.free))
        self.n += 1
        uname = "%s_%d" % (name, self.n)
        t = self.nc.alloc_sbuf_tensor_at(uname, list(shape), dtype, offset=off)
        tn = t.name
        self.live[tn] = (off, off + nbytes)
        ops = []
        keep = []
        for (lo, hi, dops) in self.dead:
            if lo < off + nbytes and off < hi:
                ops.extend(dops)
                if not (off <= lo and hi <= off + nbytes):
                    keep.append((lo, hi, dops))
            else:
                keep.append((lo, hi, dops))
        self.dead = keep
        if ops:
            uniq = {}
            for o in ops:
                uniq[id(o)] = o
            self.s.fence[tn] = list(uniq.values())
        self.peak = max(self.peak, max(h for (_, h) in self.live.values()))
        return t

    def release(self, *tensors):
        for t in tensors:
            lo, hi = self.live.pop(t.name)
            ops = self.s.collect(t.name)
            if ops:
                self.dead.append((lo, hi, ops))
            self.free.append((lo, hi))
        self.free.sort()
        merged = []
        for lo, hi in self.free:
            if merged and merged[-1][1] == lo:
                merged[-1] = (merged[-1][0], hi)
            else:
                merged.append((lo, hi))
        self.free = merged


IN_SPECS = {
    "x": [TOK, D], "xp": [TOK, D], "cvec": [128, 16], "flag": [128, 1], "amask": [128, 256],
    "w_ada": [D, 6 * D], "b_ada": [128, 96], "n1g": [128, 16], "n2g": [128, 16], "fgrow": [128, D],
    "w_in": [D, 6144], "b_q": [128, 8], "b_k": [128, 8], "b_vrow": [128, 512], "b_u": [128, 4],
    "b_ga": [128, 16], "b_gs": [128, 16], "sinks": [128, 16], "biasg": [128, 16, 256], "bandm": [128, 256],
    "lre": [128, 32], "lim": [128, 32], "lst": [128, 32],
    "bre": [128, 32, 16], "bim": [128, 32, 16], "cre": [128, 32, 16], "cim": [128, 32, 16],
    "dsk": [128, 4], "maskE": [128, 2], "maskH": [128, 2], "maskP": [128, 4], "bdmask": [128, 128],
    "identf": [128, 128],
    "w_glu": [512, 512], "b_glu": [128, 4], "w_ap": [1024, D], "w_sp": [512, D], "w_out": [D, D],
    "w_ff1": [D, 4 * D], "w_ff2": [4 * D, D],
}


def build_program(dbg=None, stop=None):
    dbg = dbg or set()
    nc = bass.Bass("TRN2", target_bir_lowering=False)
    dram = {k: nc.dram_tensor(k, v, F32, kind="ExternalInput").ap() for k, v in IN_SPECS.items()}
    out_d = nc.dram_tensor("out", [TOK, D], F32, kind="ExternalOutput").ap()
    WA_d = nc.dram_tensor("WA_d", [128, 4 * 2 * 8 * 128], BF16).ap()
    WC_d = nc.dram_tensor("WC_d", [128, 2 * 8 * 32 * 16], BF16).ap()
    FIR_d = nc.dram_tensor("FIR_d", [128, 4 * 8 * 128], BF16).ap()
    BM_d = nc.dram_tensor("BM_d", [128, 16 * 256], BF16).ap()
    dbg_d = {}

    def dbg_out(name, shape):
        dbg_d[name] = nc.dram_tensor("dbg_" + name, shape, F32, kind="ExternalOutput").ap()
        return dbg_d[name]

    s = Sched(nc)
    ar = Arena(nc, s)
    out_ops = []

    pb = [nc.alloc_psum_tensor("pb%d" % i, [128, 512], F32) for i in range(8)]

    def pk(i, sub=0):
        return ("pb%d" % i, sub)

    def retire_psum():
        for i in range(8):
            s.retire("pb%d" % i)

    def phase_end(hard=False):
        retire_psum()
        if hard or BARRIERS:
            s.barrier()

    def finalize():
        s.finish("sp", out_ops)
        s.emit()
        return nc, dbg_d

    def K(t, *sub):
        return (t.name,) + tuple(sub) if sub else t.name

    def dma(eng, out, in_, reads=(), writes=()):
        return s.add(eng, lambda e: e.dma_start(out=out, in_=in_), reads=reads, writes=writes, dma=True)

    def mm(out, lhsT, rhs, start, stop, reads, writes, skip=False):
        if skip:
            return s.add("pe", lambda e: e.matmul(out, lhsT=lhsT, rhs=rhs, start=start, stop=stop,
                                                  skip_group_check=True), reads=reads, writes=writes)
        return s.add("pe", lambda e: e.matmul(out, lhsT=lhsT, rhs=rhs, start=start, stop=stop),
                     reads=reads, writes=writes)

    def act(out, in_, func, reads, writes, bias=None, scale=None, accum_out=None):
        kw = {}
        if bias is not None:
            kw["bias"] = bias
        if scale is not None:
            kw["scale"] = scale
        if accum_out is not None:
            kw["accum_out"] = accum_out
        return s.add("act", lambda e: e.activation(out=out, in_=in_, func=func, **kw), reads=reads, writes=writes)

    def tt(out, in0, in1, op, reads, writes, eng="dve"):
        return s.add(eng, lambda e: e.tensor_tensor(out=out, in0=in0, in1=in1, op=op), reads=reads, writes=writes)

    def ts(out, in0, s1, s2, op0, op1, reads, writes, eng="dve"):
        if op1 is None:
            return s.add(eng, lambda e: e.tensor_scalar(out=out, in0=in0, scalar1=s1, scalar2=None, op0=op0),
                         reads=reads, writes=writes)
        return s.add(eng, lambda e: e.tensor_scalar(out=out, in0=in0, scalar1=s1, scalar2=s2, op0=op0, op1=op1),
                     reads=reads, writes=writes)

    def stt(out, in0, scalar, in1, op0, op1, reads, writes, eng="dve"):
        return s.add(eng, lambda e: e.scalar_tensor_tensor(out=out, in0=in0, scalar=scalar, in1=in1, op0=op0, op1=op1),
                     reads=reads, writes=writes)

    def cp(out, in_, reads, writes, eng="dve"):
        return s.add(eng, lambda e: e.tensor_copy(out=out, in_=in_), reads=reads, writes=writes)

    def recip(ap, key):
        return s.add("dve", lambda e: e.reciprocal(out=ap, in_=ap), reads=[key], writes=[key])

    def memset(t_ap, val, writes, eng="dve"):
        return s.add(eng, lambda e: e.memset(t_ap, val), writes=writes)

    def load_const(name):
        t = ar.alloc(name, IN_SPECS[name], F32)
        dma("sp", t[:], dram[name], writes=[K(t)])
        return t

    def dump(name, t_ap, shape, reads):
        d = dbg_out(name, shape)
        out_ops.append(dma("sp", d, t_ap, reads=reads))

    def dump_bf(name, t, flat_pat, shape, reads):
        n = int(np.prod(shape[1:]))
        tmp = ar.alloc("dbgtmp", [128, n], F32)
        cp(tmp[:], t[:].rearrange(flat_pat) if flat_pat else t[:], reads, [K(tmp)])
        dump(name, tmp[:], [128, n], [K(tmp)])
        phase_end()
        ar.release(tmp)

    class Slots:
        def __init__(self, n):
            self.t = [ar.alloc("wsl", [128, 16, 512], BF16) for _ in range(n)]
            self.i = 0

        def next(self):
            t = self.t[self.i % len(self.t)]
            self.i += 1
            return t

        def free(self):
            ar.release(*self.t)

    cur = {"slots": None}

    def next_slot():
        return cur["slots"].next()

    def SK(sl):
        return [K(sl, i) for i in range(4)]

    def wview(w_ap):
        return w_ap.rearrange("(kc p) n -> p kc n", p=128)

    identf = load_const("identf")
    identb = ar.alloc("identb", [128, 128], BF16)
    cp(identb[:], identf[:], [K(identf)], [K(identb)])
    onesb = ar.alloc("onesb", [128, 128], BF16)
    memset(onesb[:], 1.0, [K(onesb)])
    eps_t = ar.alloc("eps", [128, 1], F32)
    memset(eps_t[:], 1e-6, [K(eps_t)])
    flag = load_const("flag")
    sinks = load_const("sinks")
    b_q = load_const("b_q")
    b_k = load_const("b_k")
    b_u = load_const("b_u")
    b_ga = load_const("b_ga")
    b_gs = load_const("b_gs")
    b_glu = load_const("b_glu")
    dsk = load_const("dsk")
    maskP = load_const("maskP")
    bvrow = load_const("b_vrow")
    bq8 = ar.alloc("bq8", [128, 8], F32)
    ts(bq8[:], b_q[:], 0.125, None, ALU.mult, None, [K(b_q)], [K(bq8)])
    buf_ = ar.alloc("buf", [128, 4], F32)
    ts(buf_[:], b_u[:], flag[:, 0:1], None, ALU.mult, None, [K(b_u), K(flag)], [K(buf_)])
    w_in_v = wview(dram["w_in"])

    mod = ar.alloc("mod", [128, 96], F32)
    cur["slots"] = Slots(2)
    cv = ar.alloc("cv", [128, 16], F32)
    sg = ar.alloc("sg", [128, 16], F32)
    cs = ar.alloc("cs", [128, 16], BF16)
    bada = ar.alloc("bada", [128, 96], F32)
    dma("sp", cv[:], dram["cvec"], writes=[K(cv)])
    dma("sp", bada[:], dram["b_ada"], writes=[K(bada)])
    act(sg[:], cv[:], AF.Sigmoid, [K(cv)], [K(sg)])
    tt(cs[:], cv[:], sg[:], ALU.mult, [K(cv), K(sg)], [K(cs)])
    wav = wview(dram["w_ada"])

    def adaln_blocks(j0, j1, bank):
        for jb in range(j0, j1):
            sl = next_slot()
            dma("pool", sl[:], wav[:, :, jb * 512:(jb + 1) * 512], writes=SK(sl))
            for mc in range(4):
                j = jb * 4 + mc
                for kc in range(16):
                    mm(pb[bank][:, j:j + 1], sl[:, kc, mc * 128:(mc + 1) * 128], cs[:, kc:kc + 1],
                       kc == 0, kc == 15, [K(sl, mc), K(cs)], [pk(bank)])
        if j0 != 0:
            adaln_evac(j0, j1, bank)

    def adaln_evac(j0, j1, bank):
        tt(mod[:, 4 * j0:4 * j1], pb[bank][:, 4 * j0:4 * j1], bada[:, 4 * j0:4 * j1], ALU.add, [pk(bank), K(bada)],
           [K(mod, j0)])

    adaln_blocks(0, 8, 7)
    if "mod" in dbg:
        dump("mod", mod[:], [128, 96], [K(mod)])
    if stop == "p0":
        return finalize()

    n1g = load_const("n1g")
    n2g = load_const("n2g")
    gm1 = ar.alloc("gm1", [128, 16], F32)
    gm2 = ar.alloc("gm2", [128, 16], F32)
    sh1 = mod[:, 0:16]
    sh2 = mod[:, 48:64]

    def make_row(dst, src_fm_ap, src_key):
        dg = ar.alloc("dg", [128, 128], F32)
        dh = ar.alloc("dh", [128, 128], BF16)
        dl = ar.alloc("dl", [128, 128], BF16)
        dhf = ar.alloc("dhf", [128, 128], F32)
        for kc in range(16):
            bank = 4 + (kc // 4) % 2
            col = (kc % 4) * 128
            ts(dg[:], identf[:], src_fm_ap[:, kc:kc + 1], None, ALU.mult, None, [K(identf), src_key], [K(dg)])
            cp(dh[:], dg[:], [K(dg)], [K(dh)])
            cp(dhf[:], dh[:], [K(dh)], [K(dhf)])
            tt(dl[:], dg[:], dhf[:], ALU.subtract, [K(dg), K(dhf)], [K(dl)])
            mm(pb[bank][:, col:col + 128], onesb[:], dh[:], True, False, [K(onesb), K(dh)], [pk(bank, kc % 4)])
            mm(pb[bank][:, col:col + 128], onesb[:], dl[:], False, True, [K(onesb), K(dl)], [pk(bank, kc % 4)])
            if kc % 4 == 3:
                c0 = (kc // 4) * 512
                act(dst[:, c0:c0 + 512], pb[bank][:], AF.Copy, [pk(bank, i) for i in range(4)], [K(dst)])
        ar.release(dg, dh, dl, dhf)

    NSTEP = 8
    scr = ar.alloc("scr", [128, NSTEP, 16], F32)
    sci = ar.alloc("sci", [128, NSTEP, 16], F32)
    scn = ar.alloc("scn", [128, NSTEP, 16], F32)
    WA = ar.alloc("WA", [128, 4, 2, 8, 128], BF16)
    WC = ar.alloc("WC", [128, 2, 8, 32, 16], BF16)
    FIR = ar.alloc("FIR", [128, 4, 8, 128], BF16)
    lre = load_const("lre"); lim = load_const("lim"); lst = load_const("lst")
    bre = load_const("bre"); bim = load_const("bim"); cre = load_const("cre"); cim = load_const("cim")
    maskE = load_const("maskE"); maskH = load_const("maskH"); bdmask = load_const("bdmask")
    G32 = [128, 32]
    lr = ar.alloc("lr", G32, F32); dl_ = ar.alloc("dl_", G32, F32); mag = ar.alloc("mag", G32, F32)
    ang = ar.alloc("ang", G32, F32); t1 = ar.alloc("t1", G32, F32); t2 = ar.alloc("t2", G32, F32)
    ki = ar.alloc("ki", G32, I32)
    cosv = ar.alloc("cosv", G32, F32); sinv = ar.alloc("sinv", G32, F32)
    ts(lr[:], lre[:], -1e-4, None, ALU.min, None, [K(lre)], [K(lr)])
    act(dl_[:], lst[:], AF.Exp, [K(lst)], [K(dl_)])
    tt(t1[:], lr[:], dl_[:], ALU.mult, [K(lr), K(dl_)], [K(t1)])
    act(mag[:], t1[:], AF.Exp, [K(t1)], [K(mag)])
    tt(ang[:], lim[:], dl_[:], ALU.mult, [K(lim), K(dl_)], [K(ang)])

    def sin_of(dst, phase):
        ts(t1[:], ang[:], 1.0 / (2 * math.pi), (phase / (2 * math.pi)) + 8.0, ALU.mult, ALU.add, [K(ang)], [K(t1)])
        cp(ki[:], t1[:], [K(t1)], [K(ki)])
        cp(t2[:], ki[:], [K(ki)], [K(t2)])
        tt(t1[:], t1[:], t2[:], ALU.subtract, [K(t1), K(t2)], [K(t1)])
        ts(t2[:], t1[:], 0.5, None, ALU.is_ge, None, [K(t1)], [K(t2)])
        tt(t1[:], t1[:], t2[:], ALU.subtract, [K(t1), K(t2)], [K(t1)])
        ts(t2[:], t1[:], -0.5, None, ALU.is_lt, None, [K(t1)], [K(t2)])
        tt(t1[:], t1[:], t2[:], ALU.add, [K(t1), K(t2)], [K(t1)])
        act(dst[:], t1[:], AF.Sin, [K(t1)], [K(dst)], scale=float(2 * math.pi))

    sin_of(sinv, 0.0)
    sin_of(cosv, math.pi / 2)
    Ar = ar.alloc("Ar", [128, 9, 32], F32); Ai = ar.alloc("Ai", [128, 9, 32], F32)
    memset(Ar[:, 0, :], 1.0, [K(Ar, 0)])
    memset(Ai[:, 0, :], 0.0, [K(Ai, 0)])
    tt(Ar[:, 1, :], mag[:], cosv[:], ALU.mult, [K(mag), K(cosv)], [K(Ar, 1)])
    tt(Ai[:, 1, :], mag[:], sinv[:], ALU.mult, [K(mag), K(sinv)], [K(Ai, 1)])

    def cmul(or_, oi_, okr, oki, xr, xi, kxr, kxi, yr, yi, kyr, kyi, tmp, ktmp):
        tt(tmp, xi, yi, ALU.mult, [kxi, kyi], [ktmp])
        tt(or_, xr, yr, ALU.mult, [kxr, kyr], [okr])
        tt(or_, or_, tmp, ALU.subtract, [okr, ktmp], [okr])
        tt(tmp, xi, yr, ALU.mult, [kxi, kyr], [ktmp])
        tt(oi_, xr, yi, ALU.mult, [kxr, kyi], [oki])
        tt(oi_, oi_, tmp, ALU.add, [oki, ktmp], [oki])

    for k in range(2, 9):
        cmul(Ar[:, k, :], Ai[:, k, :], K(Ar, k), K(Ai, k),
             Ar[:, k - 1, :], Ai[:, k - 1, :], K(Ar, k - 1), K(Ai, k - 1),
             Ar[:, 1, :], Ai[:, 1, :], K(Ar, 1), K(Ai, 1), t1[:], K(t1))
    fr = ar.alloc("fr", G32, F32); fi = ar.alloc("fi", G32, F32); nr = ar.alloc("nr", G32, F32)
    den = ar.alloc("den", G32, F32)
    ts(nr[:], Ar[:, 1, :], -1.0, None, ALU.add, None, [K(Ar, 1)], [K(nr)])
    tt(den[:], lr[:], lr[:], ALU.mult, [K(lr)], [K(den)])
    tt(t1[:], lim[:], lim[:], ALU.mult, [K(lim)], [K(t1)])
    tt(den[:], den[:], t1[:], ALU.add, [K(den), K(t1)], [K(den)])
    recip(den[:], K(den))
    tt(fr[:], nr[:], lr[:], ALU.mult, [K(nr), K(lr)], [K(fr)])
    tt(t1[:], Ai[:, 1, :], lim[:], ALU.mult, [K(Ai, 1), K(lim)], [K(t1)])
    tt(fr[:], fr[:], t1[:], ALU.add, [K(fr), K(t1)], [K(fr)])
    tt(fr[:], fr[:], den[:], ALU.mult, [K(fr), K(den)], [K(fr)])
    tt(fi[:], Ai[:, 1, :], lr[:], ALU.mult, [K(Ai, 1), K(lr)], [K(fi)])
    tt(t1[:], nr[:], lim[:], ALU.mult, [K(nr), K(lim)], [K(t1)])
    tt(fi[:], fi[:], t1[:], ALU.subtract, [K(fi), K(t1)], [K(fi)])
    tt(fi[:], fi[:], den[:], ALU.mult, [K(fi), K(den)], [K(fi)])
    G3 = [128, 32, 16]
    bbr = ar.alloc("bbr", G3, F32); bbi = ar.alloc("bbi", G3, F32); tb = ar.alloc("tb", G3, F32)

    def bc(ap2):
        return ap2.unsqueeze(2).to_broadcast(G3)

    tt(bbr[:], bre[:], bc(fr[:]), ALU.mult, [K(bre), K(fr)], [K(bbr)])
    tt(tb[:], bim[:], bc(fi[:]), ALU.mult, [K(bim), K(fi)], [K(tb)])
    tt(bbr[:], bbr[:], tb[:], ALU.subtract, [K(bbr), K(tb)], [K(bbr)])
    tt(bbi[:], bim[:], bc(fr[:]), ALU.mult, [K(bim), K(fr)], [K(bbi)])
    tt(tb[:], bre[:], bc(fi[:]), ALU.mult, [K(bre), K(fi)], [K(tb)])
    tt(bbi[:], bbi[:], tb[:], ALU.add, [K(bbi), K(tb)], [K(bbi)])
    Xb = ar.alloc("Xb", [128, 8, 2, 512], BF16)
    xr_t = ar.alloc("xr_t", G3, F32)
    for k in range(8):
        akr = bc(Ar[:, k, :]); aki = bc(Ai[:, k, :])
        xv = Xb[:, k, 0, :].rearrange("p (g q) -> p g q", q=16)
        xvi = Xb[:, k, 1, :].rearrange("p (g q) -> p g q", q=16)
        tt(xr_t[:], bbr[:], akr, ALU.mult, [K(bbr), K(Ar, k)], [K(xr_t)])
        tt(tb[:], bbi[:], aki, ALU.mult, [K(bbi), K(Ai, k)], [K(tb)])
        tt(xv, xr_t[:], tb[:], ALU.subtract, [K(xr_t), K(tb)], [K(Xb, k, 0)])
        tt(xr_t[:], bbr[:], aki, ALU.mult, [K(bbr), K(Ai, k)], [K(xr_t)])
        tt(tb[:], bbi[:], akr, ALU.mult, [K(bbi), K(Ar, k)], [K(tb)])
        tt(xvi, xr_t[:], tb[:], ALU.add, [K(xr_t), K(tb)], [K(Xb, k, 1)])
    Cb = ar.alloc("Cb", [128, 2, 512], BF16)
    cp(Cb[:, 0, :].rearrange("p (g q) -> p g q", q=16), cre[:], [K(cre)], [K(Cb, 0)])
    ts(Cb[:, 1, :].rearrange("p (g q) -> p g q", q=16), cim[:], -1.0, None, ALU.mult, None, [K(cim)], [K(Cb, 1)])
    for c in range(4):
        for ri in range(2):
            for j in range(8):
                bank = (j % 2)
                mm(pb[bank][:, 0:64], Xb[0:64, 7 - j, ri, c * 128:(c + 1) * 128], identb[0:64, 0:64], True, True,
                   [K(Xb, 7 - j, ri), K(identb)], [pk(bank)])
                for e2 in range(2):
                    ts(WA[:, c, ri, j, e2 * 64:(e2 + 1) * 64], pb[bank][:, 0:64], maskE[:, e2:e2 + 1], None,
                       ALU.mult, None, [pk(bank), K(maskE)], [K(WA, c, ri, j, e2)])
    for c in range(4):
        for d in range(8):
            bank = 2 + (d % 2)
            mm(pb[bank][:, 0:128], Xb[0:64, d, 0, c * 128:(c + 1) * 128], Cb[0:64, 0, c * 128:(c + 1) * 128],
               True, False, [K(Xb, d, 0), K(Cb, 0)], [pk(bank)])
            mm(pb[bank][:, 0:128], Xb[0:64, d, 1, c * 128:(c + 1) * 128], Cb[0:64, 1, c * 128:(c + 1) * 128],
               False, True, [K(Xb, d, 1), K(Cb, 1)], [pk(bank)])
            tt(FIR[:, c, d, :], pb[bank][:, 0:128], bdmask[:], ALU.mult, [pk(bank), K(bdmask)], [K(FIR, c, d)])
    w1 = ar.alloc("w1", G3, F32); w2 = ar.alloc("w2", G3, F32)
    for jp in range(8):
        akr = bc(Ar[:, jp + 1, :]); aki = bc(Ai[:, jp + 1, :])
        tt(w1[:], cre[:], akr, ALU.mult, [K(cre), K(Ar, jp + 1)], [K(w1)])
        tt(tb[:], cim[:], aki, ALU.mult, [K(cim), K(Ai, jp + 1)], [K(tb)])
        tt(w1[:], w1[:], tb[:], ALU.subtract, [K(w1), K(tb)], [K(w1)])
        tt(w2[:], cre[:], aki, ALU.mult, [K(cre), K(Ai, jp + 1)], [K(w2)])
        tt(tb[:], cim[:], akr, ALU.mult, [K(cim), K(Ar, jp + 1)], [K(tb)])
        tt(w2[:], w2[:], tb[:], ALU.add, [K(w2), K(tb)], [K(w2)])
        for e2 in range(2):
            src1 = w1[:].rearrange("p (gp e) q -> p gp e q", e=2)[:, :, e2, :]
            src2 = w2[:].rearrange("p (gp e) q -> p gp e q", e=2)[:, :, e2, :]
            d1 = WC[:, 0, jp, :, :].rearrange("p (gp e) q -> p gp e q", e=2)[:, :, e2, :]
            d2 = WC[:, 1, jp, :, :].rearrange("p (gp e) q -> p gp e q", e=2)[:, :, e2, :]
            ts(d1, src1, maskH[:, e2:e2 + 1], None, ALU.mult, None, [K(w1), K(maskH)], [K(WC, 0, jp, e2)])
            ts(d2, src2, maskH[:, e2:e2 + 1], -1.0, ALU.mult, ALU.mult, [K(w2), K(maskH)], [K(WC, 1, jp, e2)])
    a8r = Ar[:, 8, :].rearrange("p (gp e) -> p gp e", e=2)
    a8i = Ai[:, 8, :].rearrange("p (gp e) -> p gp e", e=2)
    ts(scr[:, 0, :], a8r[:, :, 0], maskH[:, 0:1], None, ALU.mult, None, [K(Ar, 8), K(maskH)], [K(scr, 0)])
    stt(scr[:, 0, :], a8r[:, :, 1], maskH[:, 1:2], scr[:, 0, :], ALU.mult, ALU.add, [K(Ar, 8), K(maskH), K(scr, 0)], [K(scr, 0)])
    ts(sci[:, 0, :], a8i[:, :, 0], maskH[:, 0:1], None, ALU.mult, None, [K(Ai, 8), K(maskH)], [K(sci, 0)])
    stt(sci[:, 0, :], a8i[:, :, 1], maskH[:, 1:2], sci[:, 0, :], ALU.mult, ALU.add, [K(Ai, 8), K(maskH), K(sci, 0)], [K(sci, 0)])
    t16 = ar.alloc("t16", [128, 16], F32)
    for st_ in range(1, NSTEP):
        cmul(scr[:, st_, :], sci[:, st_, :], K(scr, st_), K(sci, st_),
             scr[:, st_ - 1, :], sci[:, st_ - 1, :], K(scr, st_ - 1), K(sci, st_ - 1),
             scr[:, st_ - 1, :], sci[:, st_ - 1, :], K(scr, st_ - 1), K(sci, st_ - 1), t16[:], K(t16))
    ts(scn[:], sci[:], -1.0, None, ALU.mult, None, [K(sci, i) for i in range(NSTEP)], [K(scn)])
    Rr = ar.alloc("Rr", [128, 16, 128], F32); Ri = ar.alloc("Ri", [128, 16, 128], F32)
    Rt = ar.alloc("Rt", [128, 16, 64], F32)
    memset(Rr[:, :, 127:128], 1.0, [K(Rr)])
    memset(Ri[:, :, 127:128], 0.0, [K(Ri)])
    for st_ in range(7):
        n_ = 1 << st_
        lo, hi = 128 - 2 * n_, 128 - n_
        shp = [128, 16, n_]
        arb = scr[:, st_, :].unsqueeze(2).to_broadcast(shp)
        aib = sci[:, st_, :].unsqueeze(2).to_broadcast(shp)
        srcr = Rr[:, :, hi:128]; srci = Ri[:, :, hi:128]
        tt(Rr[:, :, lo:hi], srcr, arb, ALU.mult, [K(Rr), K(scr, st_)], [K(Rr)])
        tt(Rt[:, :, 0:n_], srci, aib, ALU.mult, [K(Ri), K(sci, st_)], [K(Rt)])
        tt(Rr[:, :, lo:hi], Rr[:, :, lo:hi], Rt[:, :, 0:n_], ALU.subtract, [K(Rr), K(Rt)], [K(Rr)])
        tt(Ri[:, :, lo:hi], srci, arb, ALU.mult, [K(Ri), K(scr, st_)], [K(Ri)])
        tt(Rt[:, :, 0:n_], srcr, aib, ALU.mult, [K(Rr), K(sci, st_)], [K(Rt)])
        tt(Ri[:, :, lo:hi], Ri[:, :, lo:hi], Rt[:, :, 0:n_], ALU.add, [K(Ri), K(Rt)], [K(Ri)])
    phase_end()
    ar.release(Rt)
    for t_ in (WA, WC, FIR, scr, sci, scn):
        s.retire(t_.name)
    adaln_evac(0, 8, 7)
    stt(gm1[:], mod[:, 16:32], 1.0, n1g[:], ALU.add, ALU.mult, [K(mod, 0), K(n1g)], [K(gm1)])
    cur["slots"].free()
    ar.release(cv, sg)
    dma("sp", WA_d, WA[:].rearrange("p a b c d -> p (a b c d)"), reads=[K(WA)], writes=["WA_d"])
    dma("sp", WC_d, WC[:].rearrange("p a b c d -> p (a b c d)"), reads=[K(WC)], writes=["WC_d"])
    dma("sp", FIR_d, FIR[:].rearrange("p a b c -> p (a b c)"), reads=[K(FIR)], writes=["FIR_d"])
    if "ssmw" in dbg:
        dump_bf("WA", WA, "p a b c d -> p (a b c d)", [128, 8192], [K(WA)])
        dump_bf("FIR", FIR, "p a b c -> p (a b c)", [128, 4096], [K(FIR)])
        dump_bf("WC", WC, "p a b c d -> p (a b c d)", [128, 8192], [K(WC)])
        dump("scr", scr[:].rearrange("p a b -> p (a b)"), [128, NSTEP * 16], [K(scr)])
        dump("sci", sci[:].rearrange("p a b -> p (a b)"), [128, NSTEP * 16], [K(sci)])
    phase_end()
    ar.release(lre, lim, lst, bre, bim, cre, cim, maskE, maskH, bdmask, lr, dl_, mag, ang, t1, t2, ki, cosv, sinv,
               Ar, Ai, fr, fi, nr, den, bbr, bbi, tb, Xb, xr_t, Cb, w1, w2, t16, WA, WC, FIR)

    if stop == "pre":
        return finalize()
    amb = ar.alloc("amb", [128, 256], BF16)
    bg = ar.alloc("bg", [128, 16, 256], F32)
    bm = ar.alloc("bm", [128, 256], F32)
    am = ar.alloc("am", [128, 256], F32)
    bmb = ar.alloc("bmb", [128, 16, 256], BF16)
    dma("sp", bg[:], dram["biasg"], writes=[K(bg)])
    dma("sp", bm[:], dram["bandm"], writes=[K(bm)])
    dma("sp", am[:], dram["amask"], writes=[K(am)])
    tt(bmb[:], bg[:], bm[:].unsqueeze(1).to_broadcast([128, 16, 256]), ALU.add, [K(bg), K(bm)], [K(bmb)])
    cp(amb[:], am[:], [K(am)], [K(amb)])
    dma("sp", BM_d, bmb[:].rearrange("p a b -> p (a b)"), reads=[K(bmb)], writes=["BM_d"])
    phase_end()
    ar.release(bg, bm, am, bmb)

    if stop == "bias":
        return finalize()
    w_glu_sb = ar.alloc("wglu", [128, 4, 512], BF16)
    dma("pool", w_glu_sb[:], dram["w_glu"].rearrange("(kc p) n -> p kc n", p=128), writes=[K(w_glu_sb)])
    kprev = ar.alloc("kprev", [128, 8, 128], BF16)
    vprev = ar.alloc("vprev", [128, 512], BF16)
    carr = ar.alloc("carr", [128, 16], F32)
    cari = ar.alloc("cari", [128, 16], F32)
    memset(carr[:], 0.0, [K(carr)])
    memset(cari[:], 0.0, [K(cari)])

    def hk(h, kc):
        return [K(h, kc, 0), K(h, kc, 1)]

    def norm_h(xsrc, r0, hdst, c0, gm, gmk, sh, shk, xst, xh, junk, ss, hkey_fn):
        for t4 in range(4):
            xt = xst[t4 % 2]
            dma("sp", xt[:], xsrc[r0 + t4 * 128:r0 + (t4 + 1) * 128, :], writes=[K(xt)])
            memset(ss[:, t4:t4 + 1], 0.0, [K(ss, t4)])
            act(junk[:], xt[:], AF.Square, [K(xt), K(ss, t4)], [K(junk), K(ss, t4)], accum_out=ss[:, t4:t4 + 1])
            act(ss[:, t4:t4 + 1], ss[:, t4:t4 + 1], AF.Sqrt, [K(ss, t4), K(eps_t)], [K(ss, t4)], bias=eps_t[:], scale=1.0 / D)
            recip(ss[:, t4:t4 + 1], K(ss, t4))
            act(xh[:, t4, :], xt[:], AF.Copy, [K(xt), K(ss, t4)], [K(xh, t4)], scale=ss[:, t4:t4 + 1])
        for kc in range(16):
            bank = kc % 4
            for t4 in range(4):
                mm(pb[bank][:, t4 * 128:(t4 + 1) * 128], xh[:, t4, kc * 128:(kc + 1) * 128], identb[:], True, True,
                   [K(xh, t4), K(identb)], [pk(bank)])
            if kc % 2 == 0:
                ts(hdst[:, kc, c0:c0 + 512], pb[bank][:], gm[:, kc:kc + 1], sh[:, kc:kc + 1], ALU.mult, ALU.add,
                   [pk(bank), gmk, shk], [hkey_fn(kc)])
            else:
                act(hdst[:, kc, c0:c0 + 512], pb[bank][:], AF.Identity, [pk(bank), gmk, shk], [hkey_fn(kc)],
                    bias=sh[:, kc:kc + 1], scale=gm[:, kc:kc + 1])

    def mixer_front(xsrc, base):
        xst = [ar.alloc("xst%d" % i, [128, D], F32) for i in range(2)]
        xh = ar.alloc("xh", [128, 4, D], BF16)
        junk = ar.alloc("junk", [128, D], BF16)
        ss = ar.alloc("ss", [128, 4], F32)
        h = ar.alloc("h", [128, 16, ST], BF16)
        for half in range(2):
            norm_h(xsrc, base + half * 512, h, half * 512, gm1, K(gm1), sh1, K(mod, 0), xst, xh, junk, ss,
                   lambda kc, half=half: K(h, kc, half))
        ar.release(xst[0], xst[1], xh, junk, ss)
        return h

    def proj_u(h, u, pre):
        sl = next_slot()
        dma("pool", sl[:], w_in_v[:, :, 1536:2048], writes=SK(sl))
        for mc in range(4):
            for half in range(2):
                bank = (mc * 2 + half) % 4
                for kc in range(16):
                    mm(pb[bank][:], sl[:, kc, mc * 128:(mc + 1) * 128], h[:, kc, half * 512:(half + 1) * 512],
                       kc == 0, kc == 15, [K(sl, mc), K(h, kc, half)], [pk(bank)])
                if pre:
                    act(u[:, mc, half * 512:(half + 1) * 512], pb[bank][:], AF.Identity, [pk(bank), K(buf_), K(flag)],
                        [K(u, mc, half)], bias=buf_[:, mc:mc + 1], scale=flag[:, 0:1])
                else:
                    act(u[:, mc, half * 512:(half + 1) * 512], pb[bank][:], AF.Identity, [pk(bank), K(b_u)],
                        [K(u, mc, half)], bias=b_u[:, mc:mc + 1], scale=1.0)

    def proj_k(h, col_lo, ncols, dst_fn, dkey_fn):
        for half8 in range(2):
            sl = next_slot()
            memset(sl[:], 0.0, SK(sl), eng="pool")
            for cc in range(4):
                ch = half8 * 4 + cc
                g, e2 = ch // 2, ch % 2
                c0 = cc * 128 + e2 * 64
                dma("pool", sl[:, :, c0:c0 + 64], w_in_v[:, :, 1024 + g * 64:1024 + (g + 1) * 64], writes=[K(sl, cc)])
            for cc in range(4):
                ch = half8 * 4 + cc
                bank = 4 + (cc % 2)
                for kc in range(16):
                    mm(pb[bank][:, 0:ncols], sl[:, kc, cc * 128:(cc + 1) * 128], h[:, kc, col_lo:col_lo + ncols],
                       kc == 0, kc == 15, [K(sl, cc)] + hk(h, kc), [pk(bank)])
                act(dst_fn(ch), pb[bank][:, 0:ncols], AF.Identity, [pk(bank), K(b_k)], [dkey_fn(ch)],
                    bias=b_k[:, ch:ch + 1], scale=1.0)

    def proj_v(h, col_lo, nblk, dst_fn, dkey_fn):
        sl = next_slot()
        for g in range(4):
            for e2 in range(2):
                c0 = g * 128 + e2 * 64
                dma("pool", sl[:, :, c0:c0 + 64], w_in_v[:, :, 1280 + g * 64:1280 + (g + 1) * 64], writes=[K(sl, g)])
        for nb in range(nblk):
            bank = 6 + (nb % 2)
            for kc in range(16):
                mm(pb[bank][:], h[:, kc, col_lo + nb * 128:col_lo + (nb + 1) * 128], sl[:, kc, :], kc == 0, kc == 15,
                   SK(sl) + hk(h, kc), [pk(bank)])
            tt(dst_fn(nb), pb[bank][:], bvrow[:], ALU.add, [pk(bank), K(bvrow)], [dkey_fn(nb)])

    def alloc_Z():
        Za = {"r": ar.alloc("Zar", [128, 16, 129], F32), "i": ar.alloc("Zai", [128, 16, 129], F32)}
        Zb = {"r": ar.alloc("Zbr", [128, 16, 129], F32), "i": ar.alloc("Zbi", [128, 16, 129], F32)}
        return Za, Zb

    def free_Z(Za, Zb):
        ar.release(Za["r"], Za["i"], Zb["r"], Zb["i"])

    def ssm_stageA_scan(u, WAt, Za, Zb, stop_=None, reduce_only=False):
        cp(Za["r"][:, :, 0], carr[:], [K(carr)], [K(Za["r"], "c")])
        cp(Za["i"][:, :, 0], cari[:], [K(cari)], [K(Za["i"], "c")])
        if stop_ == "Aa":
            return Za
        umc = [ar.alloc("umc%d" % i, [128, 4, 8, 128], BF16) for i in range(2)]
        for c in range(4):
            if stop_ == "Ab" and c == 1:
                return Za
            um = umc[c % 2]
            usrc = u[:, c, :].rearrange("p (i j) -> p j i", j=8)
            for r in range(4):
                ts(um[:, r, :, :], usrc, maskP[:, r:r + 1], None, ALU.mult, None,
                   [K(u, c, 0), K(u, c, 1), K(maskP)], [K(um, r)])
            if stop_ == "Ac":
                return Za
            for r in range(4):
                pair = c * 4 + r
                b0_ = 4 + (pair % 2) * 2
                for ri in range(2):
                    for j in range(8):
                        mm(pb[b0_ + ri][:, 0:128], WAt[:, c, ri, j, :], um[:, r, j, :], j == 0, j == 7,
                           [K(WAt), K(um, r)], [pk(b0_ + ri)])
                cp(Za["r"][:, pair, 1:129], pb[b0_][:, 0:128], [pk(b0_)], [K(Za["r"], pair)])
                act(Za["i"][:, pair, 1:129], pb[b0_ + 1][:, 0:128], AF.Copy, [pk(b0_ + 1)], [K(Za["i"], pair)])
        retire_psum()
        for p_ in ("r", "i"):
            s.retire(Za[p_].name)
        if stop_ is not None and stop_.startswith("A"):
            return Za
        if reduce_only:
            Er = Za["r"][:, :, 1:129]; Ei = Za["i"][:, :, 1:129]
            P1 = Zb["r"][:, :, 0:128]; P2 = Zb["i"][:, :, 0:128]
            red = ar.alloc("red", [128, 4, 16], F32)
            tt(P1, Er, Rr[:], ALU.mult, [K(Za["r"]), K(Rr)], [K(Zb["r"])])
            s.add("dve", lambda e: e.reduce_sum(out=red[:, 0, :], in_=P1, axis=AX.X), reads=[K(Zb["r"])], writes=[K(red, 0)])
            tt(P2, Ei, Ri[:], ALU.mult, [K(Za["i"]), K(Ri)], [K(Zb["i"])])
            s.add("dve", lambda e: e.reduce_sum(out=red[:, 1, :], in_=P2, axis=AX.X), reads=[K(Zb["i"])], writes=[K(red, 1)])
            tt(P1, Er, Ri[:], ALU.mult, [K(Za["r"]), K(Ri), K(red, 0)], [K(Zb["r"])])
            s.add("dve", lambda e: e.reduce_sum(out=red[:, 2, :], in_=P1, axis=AX.X), reads=[K(Zb["r"])], writes=[K(red, 2)])
            tt(P2, Ei, Rr[:], ALU.mult, [K(Za["i"]), K(Rr), K(red, 1)], [K(Zb["i"])])
            s.add("dve", lambda e: e.reduce_sum(out=red[:, 3, :], in_=P2, axis=AX.X), reads=[K(Zb["i"])], writes=[K(red, 3)])
            a128r = scr[:, 7, :]; a128i = sci[:, 7, :]
            nr_ = ar.alloc("nr_", [128, 2, 16], F32)
            tt(red[:, 0, :], red[:, 0, :], red[:, 1, :], ALU.subtract, [K(red, 0), K(red, 1)], [K(red, 0)])
            tt(red[:, 2, :], red[:, 2, :], red[:, 3, :], ALU.add, [K(red, 2), K(red, 3)], [K(red, 2)])
            tt(nr_[:, 0, :], carr[:], a128r, ALU.mult, [K(carr), K(scr)], [K(nr_, 0)])
            tt(red[:, 1, :], cari[:], a128i, ALU.mult, [K(cari), K(sci), K(red, 0)], [K(red, 1)])
            tt(nr_[:, 0, :], nr_[:, 0, :], red[:, 1, :], ALU.subtract, [K(nr_, 0), K(red, 1)], [K(nr_, 0)])
            tt(nr_[:, 1, :], cari[:], a128r, ALU.mult, [K(cari), K(scr)], [K(nr_, 1)])
            tt(red[:, 3, :], carr[:], a128i, ALU.mult, [K(carr), K(sci), K(red, 2)], [K(red, 3)])
            tt(nr_[:, 1, :], nr_[:, 1, :], red[:, 3, :], ALU.add, [K(nr_, 1), K(red, 3)], [K(nr_, 1)])
            tt(carr[:], nr_[:, 0, :], red[:, 0, :], ALU.add, [K(nr_, 0), K(red, 0), K(nr_, 1)], [K(carr)])
            tt(cari[:], nr_[:, 1, :], red[:, 2, :], ALU.add, [K(nr_, 1), K(red, 2)], [K(cari)])
            ar.release(*umc, red, nr_)
            return Za
        src, dst = Za, Zb
        for st_ in range(NSTEP):
            sh_ = 1 << st_
            L = 129 - sh_
            cp(dst["r"][:, :, 0:sh_], src["r"][:, :, 0:sh_], [K(src["r"])], [K(dst["r"], "h")])
            cp(dst["i"][:, :, 0:sh_], src["i"][:, :, 0:sh_], [K(src["i"])], [K(dst["i"], "h")])
            for pair in range(16):
                ar_ = scr[:, st_, pair:pair + 1]
                stt(dst["r"][:, pair, sh_:129], src["r"][:, pair, 0:L], ar_, src["r"][:, pair, sh_:129], ALU.mult, ALU.add,
                    [K(src["r"]), K(scr)], [K(dst["r"], pair)])
                stt(dst["i"][:, pair, sh_:129], src["i"][:, pair, 0:L], ar_, src["i"][:, pair, sh_:129], ALU.mult, ALU.add,
                    [K(src["i"]), K(scr)], [K(dst["i"], pair)])
            for pair in range(16):
                ai_ = sci[:, st_, pair:pair + 1]; an_ = scn[:, st_, pair:pair + 1]
                stt(dst["r"][:, pair, sh_:129], src["i"][:, pair, 0:L], an_, dst["r"][:, pair, sh_:129], ALU.mult, ALU.add,
                    [K(src["i"]), K(scn), K(dst["r"], pair)], [K(dst["r"], pair)])
                stt(dst["i"][:, pair, sh_:129], src["r"][:, pair, 0:L], ai_, dst["i"][:, pair, sh_:129], ALU.mult, ALU.add,
                    [K(src["r"]), K(sci), K(dst["i"], pair)], [K(dst["i"], pair)])
            for p_ in ("r", "i"):
                s.retire(src[p_].name)
                s.retire(dst[p_].name)
            src, dst = dst, src
        cp(carr[:], src["r"][:, :, 128], [K(src["r"])], [K(carr)])
        cp(cari[:], src["i"][:, :, 128], [K(src["i"])], [K(cari)])
        ar.release(*umc)
        return src

    def load_WA():
        WAt = ar.alloc("WAt", [128, 4, 2, 8, 128], BF16)
        dma("sp", WAt[:].rearrange("p a b c d -> p (a b c d)"), WA_d, reads=["WA_d"], writes=[K(WAt)])
        return WAt

    for pt in range(2):
        cur["slots"] = Slots(2)
        h = mixer_front(dram["xp"], pt * ST)
        if stop == "h%d" % pt:
            if "proj" in dbg:
                for kc in range(16):
                    s.retire(h.name)
                dump_bf("h", h, "p a b -> p (a b)", [128, 16 * ST], [K(h)])
            return finalize()
        u = ar.alloc("u", [128, 4, ST], BF16)
        proj_u(h, u, True)
        if stop == "u%d" % pt:
            return finalize()
        if pt == 1:
            proj_k(h, ST - 128, 128, lambda ch: kprev[:, ch, :], lambda ch: K(kprev, ch))
            if stop == "k1":
                return finalize()
            proj_v(h, ST - 128, 1, lambda nb: vprev[:], lambda nb: K(vprev))
            if stop == "v1":
                return finalize()
        phase_end()
        cur["slots"].free()
        ar.release(h)
        WAt = load_WA()
        Za, Zb = alloc_Z()
        ssm_stageA_scan(u, WAt, Za, Zb, stop, reduce_only=True)
        if stop in ("A%d" % pt, "s%d" % pt, "Aa", "Ab", "Ac"):
            return finalize()
        phase_end()
        free_Z(Za, Zb)
        ar.release(u, WAt)
    s.retire(kprev.name)
    ar.release(Rr, Ri)
    if "carry" in dbg:
        dump("carr", carr[:], [128, 16], [K(carr)])
        dump("cari", cari[:], [128, 16], [K(cari)])
        dump_bf("kprev", kprev, "p a b -> p (a b)", [128, 1024], [K(kprev)])
        dump_bf("vprev", vprev, None, [128, 512], [K(vprev)])

    if stop == "carry":
        return finalize()
    for st in range(TOK // ST):
        base = st * ST
        d0 = (st == 0)
        sgA = ar.alloc("sgA", [128, 16, ST], BF16, top=True)
        sgS = ar.alloc("sgS", [128, 16, ST], BF16, top=True)
        cur["slots"] = Slots(2)
        h = mixer_front(dram["x"], base)
        qb = ar.alloc("qb", [128, 8, ST], BF16)
        kb = ar.alloc("kb", [128, 8, 128 + ST], BF16)
        vT = ar.alloc("vT", [128, 9, 512], BF16)
        u = ar.alloc("u", [128, 4, ST], BF16)
        cp(kb[:, :, 0:128], kprev[:], [K(kprev)], [K(kb, "p")])
        cp(vT[:, 0, :], vprev[:], [K(vprev)], [K(vT, 0)])
        for blk in range(2):
            sl = next_slot()
            dma("pool", sl[:], w_in_v[:, :, blk * 512:(blk + 1) * 512], writes=SK(sl))
            for mc in range(4):
                ch = blk * 4 + mc
                for half in range(2):
                    bank = (mc * 2 + half) % 4
                    for kc in range(16):
                        mm(pb[bank][:], sl[:, kc, mc * 128:(mc + 1) * 128], h[:, kc, half * 512:(half + 1) * 512],
                           kc == 0, kc == 15, [K(sl, mc), K(h, kc, half)], [pk(bank)])
                    act(qb[:, ch, half * 512:(half + 1) * 512], pb[bank][:], AF.Identity, [pk(bank), K(bq8)],
                        [K(qb, ch, half)], bias=bq8[:, ch:ch + 1], scale=0.125)
        for half in range(2):
            proj_k(h, half * 512, 512, lambda ch, half=half: kb[:, ch, 128 + half * 512:128 + (half + 1) * 512],
                   lambda ch, half=half: K(kb, ch, half))
        proj_v(h, 0, 8, lambda nb: vT[:, 1 + nb, :], lambda nb: K(vT, 1 + nb))
        proj_u(h, u, False)
        phase_end()
        cur["slots"].free()
        for t_ in (qb, kb, vT, u, h):
            s.retire(t_.name)
        if "proj" in dbg and d0:
            dump_bf("h", h, "p a b -> p (a b)", [128, 16 * ST], [K(h)])
            dump_bf("q", qb, "p a b -> p (a b)", [128, 8 * ST], [K(qb)])
            dump_bf("k", kb, "p a b -> p (a b)", [128, 8 * (128 + ST)], [K(kb)])
            dump_bf("v", vT, "p a b -> p (a b)", [128, 9 * 512], [K(vT)])
            dump_bf("u", u, "p a b -> p (a b)", [128, 4 * ST], [K(u)])

        biasm = ar.alloc("biasm", [128, 16, 256], BF16)
        dma("sp", biasm[:].rearrange("p a b -> p (a b)"), BM_d, reads=["BM_d"], writes=[K(biasm)])
        ptile = [ar.alloc("ptile%d" % i, [128, 256], BF16) for i in range(4)]
        pTt = [ar.alloc("pTt%d" % i, [128, 2, 128], BF16) for i in range(4)]
        dgt = [ar.alloc("dgt%d" % i, [128, 128], BF16) for i in range(4)]
        stat = [ar.alloc("stat%d" % i, [128, 4], F32) for i in range(4)]
        groups = [(nb, hg) for nb in range(8) for hg in range(8)]

        def attn_A(gi):
            nb, hg = groups[gi]
            first = (st == 0 and nb == 0)
            H = []
            for u_ in range(2):
                hd = hg * 2 + u_
                i4 = (gi % 2) * 2 + u_
                H.append(dict(hd=hd, i4=i4, bank=i4, ps=pb[i4][:, 0:256], kvh=hd // 4, ch=hd // 2, e2=hd % 2, sv=stat[i4]))
            for c_ in H:
                mm(c_["ps"], qb[:, c_["ch"], nb * 128:(nb + 1) * 128],
                   kb[:, c_["kvh"] * 2 + c_["e2"], nb * 128:nb * 128 + 256], True, False,
                   [K(qb, c_["hd"], nb), K(kb)], [pk(c_["bank"])])
                mm(c_["ps"], identb[:], biasm[:, c_["hd"], :], False, not first, [K(identb), K(biasm)], [pk(c_["bank"])])
                if first:
                    mm(c_["ps"], identb[:], amb[:], False, True, [K(identb), K(amb)], [pk(c_["bank"])])
            for c_ in H:
                sv = c_["sv"]
                s.add("dve", lambda e, sv=sv, ps=c_["ps"]: e.reduce_max(out=sv[:, 0:1], in_=ps, axis=AX.X),
                      reads=[pk(c_["bank"])], writes=[K(sv, 0)])
            for c_ in H:
                sv = c_["sv"]; hd = c_["hd"]
                ts(sv[:, 0:1], sv[:, 0:1], sinks[:, hd:hd + 1], -1.0, ALU.max, ALU.mult, [K(sv, 0), K(sinks)], [K(sv, 0)])
            for c_ in H:
                sv = c_["sv"]; i4 = c_["i4"]
                act(ptile[i4][:], c_["ps"], AF.Exp, [pk(c_["bank"]), K(sv, 0), K(sv, 1)], [K(ptile[i4]), K(sv, 1)],
                    bias=sv[:, 0:1], scale=1.0, accum_out=sv[:, 1:2])
            for c_ in H:
                sv = c_["sv"]; hd = c_["hd"]
                act(sv[:, 2:3], sinks[:, hd:hd + 1], AF.Exp, [K(sinks), K(sv, 0)], [K(sv, 2)], bias=sv[:, 0:1], scale=1.0)
            for c_ in H:
                sv = c_["sv"]
                tt(sv[:, 1:2], sv[:, 1:2], sv[:, 2:3], ALU.add, [K(sv, 1), K(sv, 2)], [K(sv, 1)])
            for c_ in H:
                sv = c_["sv"]
                recip(sv[:, 1:2], K(sv, 1))
            for k_, c_ in enumerate(H):
                sv = c_["sv"]; i4 = c_["i4"]
                ts(dgt[i4][:], identf[:], sv[:, 1:2], None, ALU.mult, None, [K(identf), K(sv, 1)], [K(dgt[i4])],
                   eng=("pool" if k_ == 1 else "dve"))

        def attn_B(gi):
            for u_ in range(2):
                i4 = (gi % 2) * 2 + u_
                bank, sub = 4, 0
                c0_ = u_ * 256
                for jk in range(2):
                    mm(pb[bank][:, c0_ + jk * 128:c0_ + (jk + 1) * 128], ptile[i4][:, jk * 128:(jk + 1) * 128],
                       dgt[i4][:], True, True, [K(ptile[i4]), K(dgt[i4])], [pk(bank, sub)])
                cp(pTt[i4][:].rearrange("p a b -> p (a b)"), pb[bank][:, c0_:c0_ + 256], [pk(bank, sub)], [K(pTt[i4])])

        def attn_C(gi):
            nb, hg = groups[gi]
            for u_ in range(2):
                hd = hg * 2 + u_
                i4 = (gi % 2) * 2 + u_
                kvh = hd // 4
                ch = hd // 2
                e2 = hd % 2
                bk_ = 5
                ps = pb[bk_][:, u_ * 128:(u_ + 1) * 128]
                for jk in range(2):
                    mm(ps, vT[:, nb + jk, kvh * 128:(kvh + 1) * 128], pTt[i4][:, jk, :], jk == 0, jk == 1,
                       [K(vT), K(pTt[i4])], [pk(bk_)])
                act(qb[e2 * 64:(e2 + 1) * 64, ch, nb * 128:(nb + 1) * 128],
                    pb[bk_][e2 * 64:(e2 + 1) * 64, u_ * 128:(u_ + 1) * 128], AF.Copy, [pk(bk_)], [K(qb, hd, nb)])

        gw = [ar.alloc("gw%d" % i, [128, 16, 256], BF16) for i in range(3)]
        units = []
        bdma = []
        done = []
        pending = []

        def gate_block(which, mb2):
            k = len(bdma)
            sl = gw[k % 3]
            col0 = (2048 if which == 0 else 4096) + mb2 * 256
            sg_t = sgA if which == 0 else sgS
            bias_t = b_ga if which == 0 else b_gs
            bdma.append(lambda: dma("pool", sl[:], w_in_v[:, :, col0:col0 + 256], writes=[K(sl)]))

            def unit(mc, half):
                m = mb2 * 2 + mc
                bank = 6 + (len(done) % 2)
                done.append(1)
                cols = slice(half * 512, (half + 1) * 512)
                for kc in range(16):
                    mm(pb[bank][:], sl[:, kc, mc * 128:(mc + 1) * 128], h[:, kc, cols], kc == 0, kc == 15,
                       [K(sl), K(h)], [pk(bank)])
                pending.append(lambda: ts(sg_t[:, m, cols], pb[bank][:], bias_t[:, m:m + 1], None, ALU.add, None,
                                          [pk(bank), K(bias_t)], [K(sg_t, m, half)]))
            for mc in range(2):
                for half in range(2):
                    units.append((k, lambda mc=mc, half=half: unit(mc, half)))

        def adaln_half(jb2):
            k = len(bdma)
            sl = gw[k % 3]
            bdma.append(lambda: dma("pool", sl[:], wav[:, :, jb2 * 256:(jb2 + 1) * 256], writes=[K(sl)]))

            def unit(mc):
                j = jb2 * 2 + mc
                for kc in range(16):
                    mm(pb[5][:, 384 + (j - 32):385 + (j - 32)], sl[:, kc, mc * 128:(mc + 1) * 128], cs[:, kc:kc + 1],
                       kc == 0, kc == 15, [K(sl), K(cs)], [pk(5)])
            for mc in range(2):
                units.append((k, lambda mc=mc: unit(mc)))

        blocks = [(0, i) for i in range(8)] + [(1, i) for i in range(8)]
        if st == 0:
            ada = list(range(16, 48))
            for (w_, mb2) in blocks:
                gate_block(w_, mb2)
                for _ in range(2):
                    adaln_half(ada.pop(0))
        else:
            for (w_, mb2) in blocks:
                gate_block(w_, mb2)
        issued = {"n": 0}

        def emit_unit():
            k, fn = units.pop(0)
            while issued["n"] < min(len(bdma), k + 3):
                bdma[issued["n"]]()
                issued["n"] += 1
            fn()

        NG = len(groups)
        per = (len(units) + NG - 1) // NG
        for gi in range(NG + 2):
            if gi < NG:
                attn_A(gi)
            prev_pending = list(pending)
            del pending[:]
            for f_ in prev_pending:
                f_()
            for _ in range(per):
                if units:
                    emit_unit()
            if 0 <= gi - 1 < NG:
                attn_B(gi - 1)
            if 0 <= gi - 2 < NG:
                attn_C(gi - 2)
        while units or pending:
            prev_pending = list(pending)
            del pending[:]
            for f_ in prev_pending:
                f_()
            if units:
                emit_unit()
        if st == 0:
            tt(mod[:, 32:96], pb[5][:, 384:448], bada[:, 32:96], ALU.add, [pk(5), K(bada)], [K(mod, 8)])
            s.retire(mod.name)
            stt(gm2[:], mod[:, 64:80], 1.0, n2g[:], ALU.add, ALU.mult, [K(mod), K(n2g)], [K(gm2)])
        cp(kprev[:], kb[:, :, ST:ST + 128], [K(kb)], [K(kprev)])
        cp(vprev[:], vT[:, 8, :], [K(vT)], [K(vprev)])
        phase_end()
        ar.release(*ptile, *pTt, *dgt, *stat, *gw)
        ar.release(kb, vT, biasm, h)
        if st == 0:
            ar.release(cs, bada)
        s.retire(qb.name)
        s.retire(sgA.name)
        s.retire(sgS.name)
        if "attn" in dbg and d0:
            dump_bf("attn", qb, "p a b -> p (a b)", [128, 8 * ST], [K(qb)])

        WAt = load_WA()
        Za, Zb = alloc_Z()
        Zf = ssm_stageA_scan(u, WAt, Za, Zb)
        Sb = {"r": ar.alloc("Sbr", [128, 16, 128], BF16), "i": ar.alloc("Sbi", [128, 16, 128], BF16)}
        cp(Sb["r"][:], Zf["r"][:, :, 0:128], [K(Zf["r"])], [K(Sb["r"])])
        act(Sb["i"][:], Zf["i"][:, :, 0:128], AF.Copy, [K(Zf["i"])], [K(Sb["i"])])
        if "ssm" in dbg and d0:
            dump("Zr", Zf["r"][:].rearrange("p a b -> p (a b)"), [128, 16 * 129], [K(Zf["r"])])
            dump("Zi", Zf["i"][:].rearrange("p a b -> p (a b)"), [128, 16 * 129], [K(Zf["i"])])
        phase_end()
        free_Z(Za, Zb)
        ar.release(WAt)
        WCt = ar.alloc("WCt", [128, 2, 8, 32, 16], BF16)
        FIRt = ar.alloc("FIRt", [128, 4, 8, 128], BF16)
        dma("sp", WCt[:].rearrange("p a b c d -> p (a b c d)"), WC_d, reads=["WC_d"], writes=[K(WCt)])
        dma("sp", FIRt[:].rearrange("p a b c -> p (a b c)"), FIR_d, reads=["FIR_d"], writes=[K(FIRt)])
        zf = ar.alloc("zf", [128, 4, ST], BF16)
        zg = ar.alloc("zg", [128, 4, ST], BF16)
        yt = [ar.alloc("yt%d" % i, [128, 512], F32) for i in range(2)]
        y2 = [ar.alloc("y2%d" % i, [128, 512], F32) for i in range(2)]
        udt = [ar.alloc("udt%d" % i, [128, 8, 64], BF16) for i in range(2)]
        ydbg = dbg_out("y", [128, 4, ST]) if ("ssm" in dbg and d0) else None
        for c in range(4):
            for half in range(2):
                pv = pb[0][:].rearrange("p (j i) -> p j i", j=8)
                udn = udt[half]
                unat = u[:, c, half * 512:(half + 1) * 512].rearrange("p (i j) -> p j i", j=8)
                cp(udn[:], unat, [K(u)], [K(udn)])
                for d in range(8):
                    mm(pv[:, d:8, :], FIRt[:, c, d, :], udn[:, 0:8 - d, :], d == 0, d == 7, [K(FIRt), K(udn)], [pk(0)], skip=True)
                for r in range(4):
                    pair = c * 4 + r
                    pvr = pb[1 + r][:].rearrange("p (j i) -> p j i", j=8)
                    for jp in range(8):
                        for ri in range(2):
                            mm(pvr[:, jp, :], WCt[:, ri, jp, c * 8:(c + 1) * 8, :].rearrange("p a b -> p (a b)"),
                               Sb["r" if ri == 0 else "i"][:, pair, half * 64:(half + 1) * 64], ri == 0, ri == 1,
                               [K(WCt), K(Sb["r"]), K(Sb["i"])], [pk(1 + r)], skip=True)
                y = yt[half]
                yv = y[:].rearrange("p (i j) -> p j i", j=8)
                stt(yv, udn[:], dsk[:, c:c + 1], pv, ALU.mult, ALU.add, [K(udn), K(dsk), pk(0)], [K(y)])
                for r in range(4):
                    stt(yv, pb[1 + r][:].rearrange("p (j i) -> p j i", j=8), maskP[:, r:r + 1], yv, ALU.mult, ALU.add,
                        [pk(1 + r), K(maskP), K(y)], [K(y)])
                if ydbg is not None:
                    out_ops.append(dma("sp", ydbg[:, c, half * 512:(half + 1) * 512], y[:], reads=[K(y)]))
                t_ = y2[half]
                tt(t_[:], y[:], y[:], ALU.mult, [K(y)], [K(t_)])
                ts(t_[:], t_[:], 0.044715, 1.0, ALU.mult, ALU.add, [K(t_)], [K(t_)])
                tt(t_[:], t_[:], y[:], ALU.mult, [K(t_), K(y)], [K(t_)])
                act(t_[:], t_[:], AF.Sigmoid, [K(t_)], [K(t_)], scale=1.5957691216057308)
                tt(zg[:, c, half * 512:(half + 1) * 512], y[:], t_[:], ALU.mult, [K(y), K(t_)], [K(zg, c, half)])
        for mc in range(4):
            for half in range(2):
                bank = 5 + (mc * 2 + half) % 3
                for kc in range(4):
                    mm(pb[bank][:], w_glu_sb[:, kc, mc * 128:(mc + 1) * 128], zg[:, kc, half * 512:(half + 1) * 512],
                       kc == 0, kc == 3, [K(w_glu_sb), K(zg, kc, half)], [pk(bank)])
                t_ = y2[half]
                act(t_[:], pb[bank][:], AF.Sigmoid, [pk(bank), K(b_glu)], [K(t_)], bias=b_glu[:, mc:mc + 1], scale=1.0)
                tt(zf[:, mc, half * 512:(half + 1) * 512], zg[:, mc, half * 512:(half + 1) * 512], t_[:], ALU.mult,
                   [K(zg, mc, half), K(t_)], [K(zf, mc, half)])
        phase_end()
        ar.release(Sb["r"], Sb["i"], zg, *yt, *y2, *udt, u, WCt, FIRt)
        s.retire(zf.name)
        if "ssm" in dbg and d0:
            dump_bf("zf", zf, "p a b -> p (a b)", [128, 4 * ST], [K(zf)])

        mg = ar.alloc("mg", [128, 16, ST], BF16, top=True)
        sA = [ar.alloc("sA%d" % i, [128, 512], F32) for i in range(2)]
        sS = [ar.alloc("sS%d" % i, [128, 512], F32) for i in range(2)]
        wsets = [ar.alloc("ws3", [128, 12, 256], BF16) for _ in range(3)]
        wap_v = wview(dram["w_ap"])
        wsp_v = wview(dram["w_sp"])
        it = 0
        for mb in range(8):
            s3 = wsets[mb % 3]
            c_lo, c_hi = mb * 256, (mb + 1) * 256
            dma("pool", s3[:, 0:8, :], wap_v[:, :, c_lo:c_hi], writes=[K(s3, "a")])
            dma("pool", s3[:, 8:12, :], wsp_v[:, :, c_lo:c_hi], writes=[K(s3, "s")])
            for mc in range(2):
                m = mb * 2 + mc
                for half in range(2):
                    b0 = (it % 4) * 2
                    it += 1
                    cols = slice(half * 512, (half + 1) * 512)
                    for kc in range(8):
                        mm(pb[b0][:], s3[:, kc, mc * 128:(mc + 1) * 128], qb[:, kc, cols], kc == 0, kc == 7,
                           [K(s3, "a"), K(qb)], [pk(b0)])
                    for kc in range(4):
                        mm(pb[b0 + 1][:], s3[:, 8 + kc, mc * 128:(mc + 1) * 128], zf[:, kc, cols], kc == 0, kc == 3,
                           [K(s3, "s"), K(zf)], [pk(b0 + 1)])
                    a_ = sA[it % 2]; g_ = sS[it % 2]
                    act(a_[:], sgA[:, m, cols], AF.Sigmoid, [K(sgA)], [K(a_)])
                    act(g_[:], sgS[:, m, cols], AF.Sigmoid, [K(sgS)], [K(g_)])
                    tt(a_[:], a_[:], pb[b0][:], ALU.mult, [K(a_), pk(b0)], [K(a_)])
                    tt(g_[:], g_[:], pb[b0 + 1][:], ALU.mult, [K(g_), pk(b0 + 1)], [K(g_)])
                    tt(mg[:, m, cols], a_[:], g_[:], ALU.add, [K(a_), K(g_)], [K(mg, m, half)])
        phase_end()
        ar.release(*wsets)
        ar.release(*sA, *sS, qb, zf, sgA, sgS)
        s.retire(mg.name)
        if "merge" in dbg and d0:
            dump_bf("merged", mg, "p a b -> p (a b)", [128, 16 * ST], [K(mg)])

        cur["slots"] = Slots(2)
        acc = ar.alloc("acc", [128, 8, D], F32)
        rowb = ar.alloc("rowb", [128, D], F32)
        tmpw = [ar.alloc("tmpw%d" % i, [128, 512], F32) for i in range(2)]
        make_row(rowb, mod[:, 32:48], K(mod))
        xv = dram["x"][base:base + ST, :].rearrange("(t p) f -> p t f", p=128)
        for t8 in range(8):
            dma("sp", acc[:, t8, :], xv[:, t8, :], writes=[K(acc, t8, cb) for cb in range(4)])
        retire_psum()
        wo_v = wview(dram["w_out"])
        it = 0
        for cb in range(4):
            sl = next_slot()
            dma("pool", sl[:], wo_v[:, :, cb * 512:(cb + 1) * 512], writes=SK(sl))
            for t8 in range(8):
                bank = it % 8
                it += 1
                for kc in range(16):
                    mm(pb[bank][:], mg[:, kc, t8 * 128:(t8 + 1) * 128], sl[:, kc, :], kc == 0, kc == 15,
                       SK(sl) + [K(mg)], [pk(bank)])
                tw = tmpw[it % 2]
                tt(tw[:], pb[bank][:], rowb[:, cb * 512:(cb + 1) * 512], ALU.mult, [pk(bank), K(rowb)], [K(tw)])
                tt(acc[:, t8, cb * 512:(cb + 1) * 512], acc[:, t8, cb * 512:(cb + 1) * 512], tw[:], ALU.add,
                   [K(acc, t8, cb), K(tw)], [K(acc, t8, cb)])
        if "x1" in dbg and d0:
            d_ = dbg_out("x1", [ST, D])
            for t8 in range(8):
                out_ops.append(dma("sp", d_[t8 * 128:(t8 + 1) * 128, :], acc[:, t8, :],
                                   reads=[K(acc, t8, cb) for cb in range(4)]))
        phase_end()
        cur["slots"].free()
        s.retire(mg.name)
        xh = ar.alloc("xh2", [128, 4, D], BF16)
        junk = ar.alloc("junk2", [128, D], BF16)
        ss = ar.alloc("ss2", [128, 8], F32)
        for half in range(2):
            for t4 in range(4):
                t8 = half * 4 + t4
                ak = [K(acc, t8, cb) for cb in range(4)]
                memset(ss[:, t8:t8 + 1], 0.0, [K(ss, t8)])
                act(junk[:], acc[:, t8, :], AF.Square, ak + [K(ss, t8)], [K(junk), K(ss, t8)], accum_out=ss[:, t8:t8 + 1])
                act(ss[:, t8:t8 + 1], ss[:, t8:t8 + 1], AF.Sqrt, [K(ss, t8), K(eps_t)], [K(ss, t8)], bias=eps_t[:], scale=1.0 / D)
                recip(ss[:, t8:t8 + 1], K(ss, t8))
                act(xh[:, t4, :], acc[:, t8, :], AF.Copy, ak + [K(ss, t8)], [K(xh, t4)], scale=ss[:, t8:t8 + 1])
            for kc in range(16):
                bank = kc % 4
                for t4 in range(4):
                    mm(pb[bank][:, t4 * 128:(t4 + 1) * 128], xh[:, t4, kc * 128:(kc + 1) * 128], identb[:], True, True,
                       [K(xh, t4), K(identb)], [pk(bank)])
                if kc % 2 == 0:
                    ts(mg[:, kc, half * 512:(half + 1) * 512], pb[bank][:], gm2[:, kc:kc + 1], sh2[:, kc:kc + 1],
                       ALU.mult, ALU.add, [pk(bank), K(gm2), K(mod)], [K(mg, kc, half)])
                else:
                    act(mg[:, kc, half * 512:(half + 1) * 512], pb[bank][:], AF.Identity, [pk(bank), K(gm2), K(mod)],
                        [K(mg, kc, half)], bias=sh2[:, kc:kc + 1], scale=gm2[:, kc:kc + 1])
        make_row(rowb, mod[:, 80:96], K(mod))
        phase_end()
        ar.release(xh, junk, ss)
        s.retire(mg.name)
        h2 = mg

        cur["slots"] = Slots(2)
        hid = ar.alloc("hid", [128, 16, ST], BF16)
        rl = [ar.alloc("rl%d" % i, [128, 512], F32) for i in range(2)]
        w1_v = wview(dram["w_ff1"])
        w2_v = dram["w_ff2"].rearrange("(g kc p) n -> p g kc n", p=128, kc=16)
        it = 0
        it2 = 0
        for g in range(4):
            for hb in range(4):
                sl = next_slot()
                dma("pool", sl[:], w1_v[:, :, g * 2048 + hb * 512:g * 2048 + (hb + 1) * 512], writes=SK(sl))
                for mc in range(4):
                    hc = hb * 4 + mc
                    for half in range(2):
                        bank = it % 4
                        it += 1
                        for kc in range(16):
                            mm(pb[bank][:], sl[:, kc, mc * 128:(mc + 1) * 128], h2[:, kc, half * 512:(half + 1) * 512],
                               kc == 0, kc == 15, [K(sl, mc), K(h2)], [pk(bank)])
                        r_ = rl[it % 2]
                        act(r_[:], pb[bank][:], AF.Relu, [pk(bank)], [K(r_)])
                        tt(hid[:, hc, half * 512:(half + 1) * 512], r_[:], r_[:], ALU.mult, [K(r_)], [K(hid, hc, half)])
            for cb in range(4):
                sl = next_slot()
                dma("pool", sl[:], w2_v[:, g, :, cb * 512:(cb + 1) * 512], writes=SK(sl))
                for t8 in range(8):
                    bank = 4 + it2 % 4
                    it2 += 1
                    for kc in range(16):
                        mm(pb[bank][:], hid[:, kc, t8 * 128:(t8 + 1) * 128], sl[:, kc, :], kc == 0, kc == 15,
                           SK(sl) + [K(hid, kc, t8 // 4)], [pk(bank)])
                    tw = tmpw[it2 % 2]
                    tt(tw[:], pb[bank][:], rowb[:, cb * 512:(cb + 1) * 512], ALU.mult, [pk(bank), K(rowb)], [K(tw)])
                    tt(acc[:, t8, cb * 512:(cb + 1) * 512], acc[:, t8, cb * 512:(cb + 1) * 512], tw[:], ALU.add,
                       [K(acc, t8, cb), K(tw)], [K(acc, t8, cb)])
        phase_end()
        cur["slots"].free()
        ar.release(hid, *rl, mg, *tmpw)
        dma("sp", rowb[:], dram["fgrow"], writes=[K(rowb)])
        junk = ar.alloc("junk3", [128, D], BF16)
        ss = ar.alloc("ss3", [128, 8], F32)
        for t8 in range(8):
            ak = [K(acc, t8, cb) for cb in range(4)]
            memset(ss[:, t8:t8 + 1], 0.0, [K(ss, t8)])
            act(junk[:], acc[:, t8, :], AF.Square, ak + [K(ss, t8)], [K(junk), K(ss, t8)], accum_out=ss[:, t8:t8 + 1])
            act(ss[:, t8:t8 + 1], ss[:, t8:t8 + 1], AF.Sqrt, [K(ss, t8), K(eps_t)], [K(ss, t8)], bias=eps_t[:], scale=1.0 / D)
            recip(ss[:, t8:t8 + 1], K(ss, t8))
            stt(acc[:, t8, :], acc[:, t8, :], ss[:, t8:t8 + 1], rowb[:], ALU.mult, ALU.mult, ak + [K(ss, t8), K(rowb)], ak)
            out_ops.append(dma("sp", out_d[base + t8 * 128:base + (t8 + 1) * 128, :], acc[:, t8, :], reads=ak))
        phase_end()
        ar.release(junk, ss, acc, rowb)

    return finalize()


def _t5_buckets():
    qi = np.arange(128)[:, None]
    ki = np.arange(256)[None, :]
    n = np.maximum(qi + 128 - ki, 0)
    max_exact = 16
    large = max_exact + (np.log(np.maximum(n, 1) / max_exact) / np.log(128 / max_exact) * (32 - max_exact)).astype(np.int32)
    large = np.minimum(large, 31)
    return np.where(n < max_exact, n, large).astype(np.int32)


def _fm(v, nchunk):
    return np.ascontiguousarray(np.asarray(v, np.float32).reshape(nchunk, 128).T)


def make_in_maps(inp):
    f32 = np.float32
    x = np.asarray(inp["x"], f32)
    c = np.asarray(inp["c"], f32)
    b_in = np.asarray(inp["b_in"], f32)[0]
    shared = {}
    shared["w_ada"] = np.ascontiguousarray(np.asarray(inp["w_ada"], f32)[0])
    shared["b_ada"] = _fm(np.asarray(inp["b_ada"], f32)[0], 96)
    shared["n1g"] = _fm(np.asarray(inp["norm1_g"], f32)[0], 16)
    shared["n2g"] = _fm(np.asarray(inp["norm2_g"], f32)[0], 16)
    shared["fgrow"] = np.ascontiguousarray(np.broadcast_to(np.asarray(inp["final_g"], f32)[None, :], (128, D)))
    shared["w_in"] = np.ascontiguousarray(np.asarray(inp["w_in"], f32)[0])
    shared["b_q"] = _fm(b_in[0:1024], 8)
    bk = np.zeros((128, 8), f32)
    for g in range(4):
        for e in range(2):
            bk[e * 64:(e + 1) * 64, g * 2 + e] = b_in[1024 + g * 64:1024 + (g + 1) * 64]
    shared["b_k"] = bk
    bv = np.zeros((512,), f32)
    for g in range(4):
        for e in range(2):
            bv[g * 128 + e * 64:g * 128 + (e + 1) * 64] = b_in[1280 + g * 64:1280 + (g + 1) * 64]
    shared["b_vrow"] = np.ascontiguousarray(np.broadcast_to(bv[None, :], (128, 512)))
    shared["b_u"] = _fm(b_in[1536:2048], 4)
    shared["b_ga"] = _fm(b_in[2048:4096], 16)
    shared["b_gs"] = _fm(b_in[4096:6144], 16)
    shared["sinks"] = np.ascontiguousarray(np.broadcast_to(np.asarray(inp["attn_sinks"], f32)[0][None, :], (128, 16)))
    bk_ = _t5_buckets()
    rb = np.asarray(inp["rel_bias"], f32)
    shared["biasg"] = np.ascontiguousarray(np.transpose(rb[bk_], (0, 2, 1)))
    qi = np.arange(128)[:, None]
    ki = np.arange(256)[None, :]
    dist = qi + 128 - ki
    band = (dist >= 0) & (dist < 128)
    shared["bandm"] = np.where(band, 0.0, NEG).astype(f32)
    rep2 = lambda a: np.ascontiguousarray(np.concatenate([a, a], axis=0))
    shared["lre"] = rep2(np.asarray(inp["lambda_re"], f32)[0].T)
    shared["lim"] = rep2(np.asarray(inp["lambda_im"], f32)[0].T)
    shared["lst"] = np.ascontiguousarray(np.broadcast_to(np.asarray(inp["log_step"], f32)[0][None, :], (128, 32)))
    shared["bre"] = rep2(np.transpose(np.asarray(inp["ssm_b_re"], f32)[0], (1, 0, 2)))
    shared["bim"] = rep2(np.transpose(np.asarray(inp["ssm_b_im"], f32)[0], (1, 0, 2)))
    shared["cre"] = rep2(np.transpose(np.asarray(inp["ssm_c_re"], f32)[0], (2, 0, 1)))
    shared["cim"] = rep2(np.transpose(np.asarray(inp["ssm_c_im"], f32)[0], (2, 0, 1)))
    shared["dsk"] = _fm(np.asarray(inp["ssm_d"], f32)[0], 4)
    p = np.arange(128)
    shared["maskE"] = np.stack([((p // 16) % 2 == e) for e in range(2)], 1).astype(f32)
    shared["maskH"] = np.stack([(p // 64 == e) for e in range(2)], 1).astype(f32)
    shared["maskP"] = np.stack([(p // 32 == r) for r in range(4)], 1).astype(f32)
    shared["bdmask"] = (p[:, None] // 16 == p[None, :] // 16).astype(f32)
    shared["identf"] = np.eye(128, dtype=f32)
    shared["w_glu"] = np.ascontiguousarray(np.asarray(inp["w_glu"], f32)[0])
    shared["b_glu"] = _fm(np.asarray(inp["b_glu"], f32)[0], 4)
    shared["w_ap"] = np.ascontiguousarray(np.asarray(inp["w_attn_proj"], f32)[0])
    shared["w_sp"] = np.ascontiguousarray(np.asarray(inp["w_ssm_proj"], f32)[0])
    shared["w_out"] = np.ascontiguousarray(np.asarray(inp["w_out"], f32)[0])
    shared["w_ff1"] = np.ascontiguousarray(np.asarray(inp["w_ff1"], f32)[0])
    shared["w_ff2"] = np.ascontiguousarray(np.asarray(inp["w_ff2"], f32)[0])
    maps = []
    zeros_x = np.zeros((TOK, D), f32)
    am0 = np.zeros((128, 256), f32)
    am1 = np.zeros((128, 256), f32)
    am1[:, 0:128] = NEG
    for core in range(NCORES):
        b, hf = core // 2, core % 2
        m = dict(shared)
        m["x"] = np.ascontiguousarray(x[b, hf * TOK:(hf + 1) * TOK])
        m["xp"] = np.ascontiguousarray(x[b, 0:TOK]) if hf == 1 else zeros_x
        m["cvec"] = _fm(c[b], 16)
        m["flag"] = np.full((128, 1), float(hf), f32)
        m["amask"] = am0 if hf == 1 else am1
        maps.append(m)
    return maps


_CACHE = {}


def kernel(**inputs):
    if "nc" not in _CACHE:
        _CACHE["nc"] = build_program()[0]
    nc = _CACHE["nc"]
    maps = make_in_maps(inputs)
    res = run_bass_kernel_spmd(nc, maps, core_ids=list(range(NCORES)))
    out = np.empty((4, 4096, D), np.float32)
    for core in range(NCORES):
        b, hf = core // 2, core % 2
        out[b, hf * TOK:(hf + 1) * TOK] = np.asarray(res.results[core]["out"], np.float32)
    return out
```

```python
import contextlib
import math
import numpy as np
import concourse.bass as bass
import concourse.mybir as mybir
from concourse.bass_utils import run_bass_kernel_spmd

F32 = mybir.dt.float32
BF16 = mybir.dt.bfloat16
I32 = mybir.dt.int32
AF = mybir.ActivationFunctionType
ALU = mybir.AluOpType
AX = mybir.AxisListType

ENGS = ("pe", "act", "dve", "pool", "sp")
NCORES = 8
TOK = 2048
ST = 1024
D = 2048
NEG = -30000.0
BARRIERS = False


class Op:
    __slots__ = ("eng", "fn", "deps", "needs_inc", "count", "is_dma", "dsem", "dval")

    def __init__(self, eng, fn, is_dma):
        self.eng = eng
        self.fn = fn
        self.deps = []
        self.needs_inc = False
        self.count = 0
        self.is_dma = is_dma
        self.dsem = None
        self.dval = 0


class Sched:
    NDSEM = 8

    def __init__(self, nc):
        self.nc = nc
        self.ops = {e: [] for e in ENGS}
        self.lastw = {}
        self.readers = {}
        self.fence = {}
        self.gfence = []
        self.dma_rr = {e: 0 for e in ENGS}
        self.dma_cnt = {}
        self.dma_last = {}
        self.final = []

    @staticmethod
    def _buf(key):
        return key[0] if isinstance(key, tuple) else key

    @staticmethod
    def _is_psum(k):
        b = k[0] if isinstance(k, tuple) else k
        return isinstance(b, str) and len(b) == 3 and b.startswith("pb")

    def add(self, eng, fn, reads=(), writes=(), dma=False):
        pkeys = []
        for k in list(reads) + list(writes):
            if self._is_psum(k):
                kk = ((k[0] if isinstance(k, tuple) else k), "x")
                if kk not in pkeys:
                    pkeys.append(kk)
        reads = [k for k in reads if not self._is_psum(k)]
        writes = [k for k in writes if not self._is_psum(k)] + pkeys
        op = Op(eng, fn, dma)
        deps = list(self.gfence)
        for k in reads:
            w = self.lastw.get(k)
            if w is not None:
                deps.append(w)
            deps.extend(self.fence.get(self._buf(k), ()))
        for k in writes:
            w = self.lastw.get(k)
            if w is not None:
                deps.append(w)
            deps.extend(self.readers.get(k, ()))
            deps.extend(self.fence.get(self._buf(k), ()))
        if dma:
            slot = (eng, self.dma_rr[eng] % self.NDSEM)
            self.dma_rr[eng] += 1
            self.dma_cnt[slot] = self.dma_cnt.get(slot, 0) + 1
            op.dsem = slot
            op.dval = 16 * self.dma_cnt[slot]
            prev = self.dma_last.get(slot)
            if prev is not None:
                deps.append(prev)
            self.dma_last[slot] = op
        seen = set()
        for d in deps:
            if id(d) in seen:
                continue
            seen.add(id(d))
            if (not d.is_dma) and d.eng == eng and eng == "pe":
                continue
            if not d.is_dma:
                d.needs_inc = True
            op.deps.append(d)
        for k in reads:
            self.readers.setdefault(k, []).append(op)
        for k in writes:
            self.lastw[k] = op
            self.readers[k] = []
        self.ops[eng].append(op)
        return op

    def collect(self, buf):
        ops = list(self.fence.pop(buf, ()))
        for k in list(self.lastw.keys()):
            if self._buf(k) == buf:
                ops.append(self.lastw.pop(k))
        for k in list(self.readers.keys()):
            if self._buf(k) == buf:
                ops.extend(self.readers.pop(k))
        uniq = {}
        for o in ops:
            uniq[id(o)] = o
        return list(uniq.values())

    def retire(self, buf):
        self.fence[buf] = self.collect(buf)

    def barrier(self):
        ops = []
        for e in ENGS:
            for o in reversed(self.ops[e]):
                if not o.is_dma:
                    ops.append(o)
                    break
        ops.extend(self.dma_last.values())
        self.gfence = ops

    def finish(self, eng, ops):
        self.final.append((eng, list(ops)))

    def emit(self):
        nc = self.nc
        for eng, ops in self.final:
            for o in ops:
                if not o.is_dma:
                    o.needs_inc = True
        for e in ENGS:
            c = 0
            for op in self.ops[e]:
                if op.needs_inc and not op.is_dma:
                    c += 1
                    op.count = c
        with contextlib.ExitStack() as st:
            esem = {e: st.enter_context(nc.semaphore("s_" + e)) for e in ENGS}
            dsem = {}
            for slot in self.dma_cnt:
                dsem[slot] = st.enter_context(nc.semaphore("d_%s_%d" % slot))
            block = st.enter_context(nc.Block())

            def comp(d):
                if d.is_dma:
                    return dsem[d.dsem], d.dval
                return esem[d.eng], d.count

            def run(e, engine):
                known = {}

                def waits(deps):
                    for d in deps:
                        s, v = comp(d)
                        if known.get(id(s), 0) < v:
                            engine.wait_ge(s, v)
                            known[id(s)] = v

                for op in self.ops[e]:
                    waits(op.deps)
                    ins = op.fn(engine)
                    if op.is_dma:
                        ins.then_inc(dsem[op.dsem], 16)
                    elif op.needs_inc:
                        ins.then_inc(esem[e], 1)
                for eng, ops in self.final:
                    if eng == e:
                        waits(ops)

            block.tensor(lambda t: run("pe", t))
            block.scalar(lambda t: run("act", t))
            block.vector(lambda t: run("dve", t))
            block.gpsimd(lambda t: run("pool", t))
            block.sync(lambda t: run("sp", t))


class Arena:
    LO, HI = 16640, 229120

    def __init__(self, nc, sched):
        self.nc = nc
        self.s = sched
        self.free = [(self.LO, self.HI)]
        self.live = {}
        self.dead = []
        self.n = 0
        self.peak = 0

    def alloc(self, name, shape, dtype, top=False):
        nbytes = int(np.prod(shape[1:])) * mybir.dt.size(dtype)
        nbytes = (nbytes + 63) // 64 * 64
        order = list(enumerate(self.free))
        if top:
            order = order[::-1]
        for i, (lo, hi) in order:
            if hi - lo >= nbytes:
                if top:
                    off = hi - nbytes
                    if hi - lo == nbytes:
                        self.free.pop(i)
                    else:
                        self.free[i] = (lo, hi - nbytes)
                else:
                    off = lo
                    if hi - lo == nbytes:
                        self.free.pop(i)
                    else:
                        self.free[i] = (lo + nbytes, hi)
                break
        else:
            raise MemoryError("SBUF arena full allocating %s (%d B); free=%s" % (name, nbytes, self.free))
        self.n += 1
        uname = "%s_%d" % (name, self.n)
        t = self.nc.alloc_sbuf_tensor_at(uname, list(shape), dtype, offset=off)
        tn = t.name
        self.live[tn] = (off, off + nbytes)
        ops = []
        keep = []
        for (lo, hi, dops) in self.dead:
            if lo < off + nbytes and off < hi:
                ops.extend(dops)
                if not (off <= lo and hi <= off + nbytes):
                    keep.append((lo, hi, dops))
            else:
                keep.append((lo, hi, dops))
        self.dead = keep
        if ops:
            uniq = {}
            for o in ops:
                uniq[id(o)] = o
            self.s.fence[tn] = list(uniq.values())
        self.peak = max(self.peak, max(h for (_, h) in self.live.values()))
        return t

    def release(self, *tensors):
        for t in tensors:
            lo, hi = self.live.pop(t.name)
            ops = self.s.collect(t.name)
            if ops:
                self.dead.append((lo, hi, ops))
            self.free.append((lo, hi))
        self.free.sort()
        merged = []
        for lo, hi in self.free:
            if merged and merged[-1][1] == lo:
                merged[-1] = (merged[-1][0], hi)
            else:
                merged.append((lo, hi))
        self.free = merged


IN_SPECS = {
    "x": [TOK, D], "xp": [TOK, D], "cvec": [128, 16], "flag": [128, 1], "amask": [128, 256],
    "w_ada": [D, 6 * D], "b_ada": [128, 96], "n1g": [128, 16], "n2g": [128, 16], "fgrow": [128, D],
    "w_in": [D, 6144], "b_q": [128, 8], "b_k": [128, 8], "b_vrow": [128, 512], "b_u": [128, 4],
    "b_ga": [128, 16], "b_gs": [128, 16], "sinks": [128, 16], "biasg": [128, 16, 256], "bandm": [128, 256],
    "lre": [128, 32], "lim": [128, 32], "lst": [128, 32],
    "bre": [128, 32, 16], "bim": [128, 32, 16], "cre": [128, 32, 16], "cim": [128, 32, 16],
    "dsk": [128, 4], "maskE": [128, 2], "maskH": [128, 2], "maskP": [128, 4], "bdmask": [128, 128],
    "identf": [128, 128],
    "w_glu": [512, 512], "b_glu": [128, 4], "w_ap": [1024, D], "w_sp": [512, D], "w_out": [D, D],
    "w_ff1": [D, 4 * D], "w_ff2": [4 * D, D],
}


def build_program(dbg=None, stop=None):
    dbg = dbg or set()
    nc = bass.Bass("TRN2", target_bir_lowering=False)
    dram = {k: nc.dram_tensor(k, v, F32, kind="ExternalInput").ap() for k, v in IN_SPECS.items()}
    out_d = nc.dram_tensor("out", [TOK, D], F32, kind="ExternalOutput").ap()
    WA_d = nc.dram_tensor("WA_d", [128, 4 * 2 * 8 * 128], BF16).ap()
    WC_d = nc.dram_tensor("WC_d", [128, 2 * 8 * 32 * 16], BF16).ap()
    FIR_d = nc.dram_tensor("FIR_d", [128, 4 * 8 * 128], BF16).ap()
    BM_d = nc.dram_tensor("BM_d", [128, 16 * 256], BF16).ap()
    dbg_d = {}

    def dbg_out(name, shape):
        dbg_d[name] = nc.dram_tensor("dbg_" + name, shape, F32, kind="ExternalOutput").ap()
        return dbg_d[name]

    s = Sched(nc)
    ar = Arena(nc, s)
    out_ops = []

    pb = [nc.alloc_psum_tensor("pb%d" % i, [128, 512], F32) for i in range(8)]

    def pk(i, sub=0):
        return ("pb%d" % i, sub)

    def retire_psum():
        for i in range(8):
            s.retire("pb%d" % i)

    def phase_end(hard=False):
        retire_psum()
        if hard or BARRIERS:
            s.barrier()

    def finalize():
        s.finish("sp", out_ops)
        s.emit()
        return nc, dbg_d

    def K(t, *sub):
        return (t.name,) + tuple(sub) if sub else t.name

    def dma(eng, out, in_, reads=(), writes=()):
        return s.add(eng, lambda e: e.dma_start(out=out, in_=in_), reads=reads, writes=writes, dma=True)

    def mm(out, lhsT, rhs, start, stop, reads, writes, skip=False):
        if skip:
            return s.add("pe", lambda e: e.matmul(out, lhsT=lhsT, rhs=rhs, start=start, stop=stop,
                                                  skip_group_check=True), reads=reads, writes=writes)
        return s.add("pe", lambda e: e.matmul(out, lhsT=lhsT, rhs=rhs, start=start, stop=stop),
                     reads=reads, writes=writes)

    def act(out, in_, func, reads, writes, bias=None, scale=None, accum_out=None):
        kw = {}
        if bias is not None:
            kw["bias"] = bias
        if scale is not None:
            kw["scale"] = scale
        if accum_out is not None:
            kw["accum_out"] = accum_out
        return s.add("act", lambda e: e.activation(out=out, in_=in_, func=func, **kw), reads=reads, writes=writes)

    def tt(out, in0, in1, op, reads, writes, eng="dve"):
        return s.add(eng, lambda e: e.tensor_tensor(out=out, in0=in0, in1=in1, op=op), reads=reads, writes=writes)

    def ts(out, in0, s1, s2, op0, op1, reads, writes, eng="dve"):
        if op1 is None:
            return s.add(eng, lambda e: e.tensor_scalar(out=out, in0=in0, scalar1=s1, scalar2=None, op0=op0),
                         reads=reads, writes=writes)
        return s.add(eng, lambda e: e.tensor_scalar(out=out, in0=in0, scalar1=s1, scalar2=s2, op0=op0, op1=op1),
                     reads=reads, writes=writes)

    def stt(out, in0, scalar, in1, op0, op1, reads, writes, eng="dve"):
        return s.add(eng, lambda e: e.scalar_tensor_tensor(out=out, in0=in0, scalar=scalar, in1=in1, op0=op0, op1=op1),
                     reads=reads, writes=writes)

    def cp(out, in_, reads, writes, eng="dve"):
        return s.add(eng, lambda e: e.tensor_copy(out=out, in_=in_), reads=reads, writes=writes)

    def recip(ap, key):
        return s.add("dve", lambda e: e.reciprocal(out=ap, in_=ap), reads=[key], writes=[key])

    def memset(t_ap, val, writes, eng="dve"):
        return s.add(eng, lambda e: e.memset(t_ap, val), writes=writes)

    def load_const(name):
        t = ar.alloc(name, IN_SPECS[name], F32)
        dma("sp", t[:], dram[name], writes=[K(t)])
        return t

    def dump(name, t_ap, shape, reads):
        d = dbg_out(name, shape)
        out_ops.append(dma("sp", d, t_ap, reads=reads))

    def dump_bf(name, t, flat_pat, shape, reads):
        n = int(np.prod(shape[1:]))
        tmp = ar.alloc("dbgtmp", [128, n], F32)
        cp(tmp[:], t[:].rearrange(flat_pat) if flat_pat else t[:], reads, [K(tmp)])
        dump(name, tmp[:], [128, n], [K(tmp)])
        phase_end()
        ar.release(tmp)

    class Slots:
        def __init__(self, n):
            self.t = [ar.alloc("wsl", [128, 16, 512], BF16) for _ in range(n)]
            self.i = 0

        def next(self):
            t = self.t[self.i % len(self.t)]
            self.i += 1
            return t

        def free(self):
            ar.release(*self.t)

    cur = {"slots": None}

    def next_slot():
        return cur["slots"].next()

    def SK(sl):
        return [K(sl, i) for i in range(4)]

    def wview(w_ap):
        return w_ap.rearrange("(kc p) n -> p kc n", p=128)

    identf = load_const("identf")
    identb = ar.alloc("identb", [128, 128], BF16)
    cp(identb[:], identf[:], [K(identf)], [K(identb)])
    onesb = ar.alloc("onesb", [128, 128], BF16)
    memset(onesb[:], 1.0, [K(onesb)])
    eps_t = ar.alloc("eps", [128, 1], F32)
    memset(eps_t[:], 1e-6, [K(eps_t)])
    flag = load_const("flag")
    sinks = load_const("sinks")
    b_q = load_const("b_q")
    b_k = load_const("b_k")
    b_u = load_const("b_u")
    b_ga = load_const("b_ga")
    b_gs = load_const("b_gs")
    b_glu = load_const("b_glu")
    dsk = load_const("dsk")
    maskP = load_const("maskP")
    bvrow = load_const("b_vrow")
    bq8 = ar.alloc("bq8", [128, 8], F32)
    ts(bq8[:], b_q[:], 0.125, None, ALU.mult, None, [K(b_q)], [K(bq8)])
    buf_ = ar.alloc("buf", [128, 4], F32)
    ts(buf_[:], b_u[:], flag[:, 0:1], None, ALU.mult, None, [K(b_u), K(flag)], [K(buf_)])
    w_in_v = wview(dram["w_in"])

    mod = ar.alloc("mod", [128, 96], F32)
    cur["slots"] = Slots(2)
    cv = ar.alloc("cv", [128, 16], F32)
    sg = ar.alloc("sg", [128, 16], F32)
    cs = ar.alloc("cs", [128, 16], BF16)
    bada = ar.alloc("bada", [128, 96], F32)
    dma("sp", cv[:], dram["cvec"], writes=[K(cv)])
    dma("sp", bada[:], dram["b_ada"], writes=[K(bada)])
    act(sg[:], cv[:], AF.Sigmoid, [K(cv)], [K(sg)])
    tt(cs[:], cv[:], sg[:], ALU.mult, [K(cv), K(sg)], [K(cs)])
    wav = wview(dram["w_ada"])

    def adaln_blocks(j0, j1, bank):
        for jb in range(j0, j1):
            sl = next_slot()
            dma("pool", sl[:], wav[:, :, jb * 512:(jb + 1) * 512], writes=SK(sl))
            for mc in range(4):
                j = jb * 4 + mc
                for kc in range(16):
                    mm(pb[bank][:, j:j + 1], sl[:, kc, mc * 128:(mc + 1) * 128], cs[:, kc:kc + 1],
                       kc == 0, kc == 15, [K(sl, mc), K(cs)], [pk(bank)])
        if j0 != 0:
            adaln_evac(j0, j1, bank)

    def adaln_evac(j0, j1, bank):
        tt(mod[:, 4 * j0:4 * j1], pb[bank][:, 4 * j0:4 * j1], bada[:, 4 * j0:4 * j1], ALU.add, [pk(bank), K(bada)],
           [K(mod, j0)])

    adaln_blocks(0, 8, 7)
    if "mod" in dbg:
        dump("mod", mod[:], [128, 96], [K(mod)])
    if stop == "p0":
        return finalize()

    n1g = load_const("n1g")
    n2g = load_const("n2g")
    gm1 = ar.alloc("gm1", [128, 16], F32)
    gm2 = ar.alloc("gm2", [128, 16], F32)
    sh1 = mod[:, 0:16]
    sh2 = mod[:, 48:64]

    def make_row(dst, src_fm_ap, src_key):
        dg = ar.alloc("dg", [128, 128], F32)
        dh = ar.alloc("dh", [128, 128], BF16)
        dl = ar.alloc("dl", [128, 128], BF16)
        dhf = ar.alloc("dhf", [128, 128], F32)
        for kc in range(16):
            bank = 4 + (kc // 4) % 2
            col = (kc % 4) * 128
            ts(dg[:], identf[:], src_fm_ap[:, kc:kc + 1], None, ALU.mult, None, [K(identf), src_key], [K(dg)])
            cp(dh[:], dg[:], [K(dg)], [K(dh)])
            cp(dhf[:], dh[:], [K(dh)], [K(dhf)])
            tt(dl[:], dg[:], dhf[:], ALU.subtract, [K(dg), K(dhf)], [K(dl)])
            mm(pb[bank][:, col:col + 128], onesb[:], dh[:], True, False, [K(onesb), K(dh)], [pk(bank, kc % 4)])
            mm(pb[bank][:, col:col + 128], onesb[:], dl[:], False, True, [K(onesb), K(dl)], [pk(bank, kc % 4)])
            if kc % 4 == 3:
                c0 = (kc // 4) * 512
                act(dst[:, c0:c0 + 512], pb[bank][:], AF.Copy, [pk(bank, i) for i in range(4)], [K(dst)])
        ar.release(dg, dh, dl, dhf)

    NSTEP = 8
    scr = ar.alloc("scr", [128, NSTEP, 16], F32)
    sci = ar.alloc("sci", [128, NSTEP, 16], F32)
    scn = ar.alloc("scn", [128, NSTEP, 16], F32)
    WA = ar.alloc("WA", [128, 4, 2, 8, 128], BF16)
    WC = ar.alloc("WC", [128, 2, 8, 32, 16], BF16)
    FIR = ar.alloc("FIR", [128, 4, 8, 128], BF16)
    lre = load_const("lre"); lim = load_const("lim"); lst = load_const("lst")
    bre = load_const("bre"); bim = load_const("bim"); cre = load_const("cre"); cim = load_const("cim")
    maskE = load_const("maskE"); maskH = load_const("maskH"); bdmask = load_const("bdmask")
    G32 = [128, 32]
    lr = ar.alloc("lr", G32, F32); dl_ = ar.alloc("dl_", G32, F32); mag = ar.alloc("mag", G32, F32)
    ang = ar.alloc("ang", G32, F32); t1 = ar.alloc("t1", G32, F32); t2 = ar.alloc("t2", G32, F32)
    ki = ar.alloc("ki", G32, I32)
    cosv = ar.alloc("cosv", G32, F32); sinv = ar.alloc("sinv", G32, F32)
    ts(lr[:], lre[:], -1e-4, None, ALU.min, None, [K(lre)], [K(lr)])
    act(dl_[:], lst[:], AF.Exp, [K(lst)], [K(dl_)])
    tt(t1[:], lr[:], dl_[:], ALU.mult, [K(lr), K(dl_)], [K(t1)])
    act(mag[:], t1[:], AF.Exp, [K(t1)], [K(mag)])
    tt(ang[:], lim[:], dl_[:], ALU.mult, [K(lim), K(dl_)], [K(ang)])

    def sin_of(dst, phase):
        ts(t1[:], ang[:], 1.0 / (2 * math.pi), (phase / (2 * math.pi)) + 8.0, ALU.mult, ALU.add, [K(ang)], [K(t1)])
        cp(ki[:], t1[:], [K(t1)], [K(ki)])
        cp(t2[:], ki[:], [K(ki)], [K(t2)])
        tt(t1[:], t1[:], t2[:], ALU.subtract, [K(t1), K(t2)], [K(t1)])
        ts(t2[:], t1[:], 0.5, None, ALU.is_ge, None, [K(t1)], [K(t2)])
        tt(t1[:], t1[:], t2[:], ALU.subtract, [K(t1), K(t2)], [K(t1)])
        ts(t2[:], t1[:], -0.5, None, ALU.is_lt, None, [K(t1)], [K(t2)])
        tt(t1[:], t1[:], t2[:], ALU.add, [K(t1), K(t2)], [K(t1)])
        act(dst[:], t1[:], AF.Sin, [K(t1)], [K(dst)], scale=float(2 * math.pi))

    sin_of(sinv, 0.0)
    sin_of(cosv, math.pi / 2)
    Ar = ar.alloc("Ar", [128, 9, 32], F32); Ai = ar.alloc("Ai", [128, 9, 32], F32)
    memset(Ar[:, 0, :], 1.0, [K(Ar, 0)])
    memset(Ai[:, 0, :], 0.0, [K(Ai, 0)])
    tt(Ar[:, 1, :], mag[:], cosv[:], ALU.mult, [K(mag), K(cosv)], [K(Ar, 1)])
    tt(Ai[:, 1, :], mag[:], sinv[:], ALU.mult, [K(mag), K(sinv)], [K(Ai, 1)])

    def cmul(or_, oi_, okr, oki, xr, xi, kxr, kxi, yr, yi, kyr, kyi, tmp, ktmp):
        tt(tmp, xi, yi, ALU.mult, [kxi, kyi], [ktmp])
        tt(or_, xr, yr, ALU.mult, [kxr, kyr], [okr])
        tt(or_, or_, tmp, ALU.subtract, [okr, ktmp], [okr])
        tt(tmp, xi, yr, ALU.mult, [kxi, kyr], [ktmp])
        tt(oi_, xr, yi, ALU.mult, [kxr, kyi], [oki])
        tt(oi_, oi_, tmp, ALU.add, [oki, ktmp], [oki])

    for k in range(2, 9):
        cmul(Ar[:, k, :], Ai[:, k, :], K(Ar, k), K(Ai, k),
             Ar[:, k - 1, :], Ai[:, k - 1, :], K(Ar, k - 1), K(Ai, k - 1),
             Ar[:, 1, :], Ai[:, 1, :], K(Ar, 1), K(Ai, 1), t1[:], K(t1))
    fr = ar.alloc("fr", G32, F32); fi = ar.alloc("fi", G32, F32); nr = ar.alloc("nr", G32, F32)
    den = ar.alloc("den", G32, F32)
    ts(nr[:], Ar[:, 1, :], -1.0, None, ALU.add, None, [K(Ar, 1)], [K(nr)])
    tt(den[:], lr[:], lr[:], ALU.mult, [K(lr)], [K(den)])
    tt(t1[:], lim[:], lim[:], ALU.mult, [K(lim)], [K(t1)])
    tt(den[:], den[:], t1[:], ALU.add, [K(den), K(t1)], [K(den)])
    recip(den[:], K(den))
    tt(fr[:], nr[:], lr[:], ALU.mult, [K(nr), K(lr)], [K(fr)])
    tt(t1[:], Ai[:, 1, :], lim[:], ALU.mult, [K(Ai, 1), K(lim)], [K(t1)])
    tt(fr[:], fr[:], t1[:], ALU.add, [K(fr), K(t1)], [K(fr)])
    tt(fr[:], fr[:], den[:], ALU.mult, [K(fr), K(den)], [K(fr)])
    tt(fi[:], Ai[:, 1, :], lr[:], ALU.mult, [K(Ai, 1), K(lr)], [K(fi)])
    tt(t1[:], nr[:], lim[:], ALU.mult, [K(nr), K(lim)], [K(t1)])
    tt(fi[:], fi[:], t1[:], ALU.subtract, [K(fi), K(t1)], [K(fi)])
    tt(fi[:], fi[:], den[:], ALU.mult, [K(fi), K(den)], [K(fi)])
    G3 = [128, 32, 16]
    bbr = ar.alloc("bbr", G3, F32); bbi = ar.alloc("bbi", G3, F32); tb = ar.alloc("tb", G3, F32)

    def bc(ap2):
        return ap2.unsqueeze(2).to_broadcast(G3)

    tt(bbr[:], bre[:], bc(fr[:]), ALU.mult, [K(bre), K(fr)], [K(bbr)])
    tt(tb[:], bim[:], bc(fi[:]), ALU.mult, [K(bim), K(fi)], [K(tb)])
    tt(bbr[:], bbr[:], tb[:], ALU.subtract, [K(bbr), K(tb)], [K(bbr)])
    tt(bbi[:], bim[:], bc(fr[:]), ALU.mult, [K(bim), K(fr)], [K(bbi)])
    tt(tb[:], bre[:], bc(fi[:]), ALU.mult, [K(bre), K(fi)], [K(tb)])
    tt(bbi[:], bbi[:], tb[:], ALU.add, [K(bbi), K(tb)], [K(bbi)])
    Xb = ar.alloc("Xb", [128, 8, 2, 512], BF16)
    xr_t = ar.alloc("xr_t", G3, F32)
    for k in range(8):
        akr = bc(Ar[:, k, :]); aki = bc(Ai[:, k, :])
        xv = Xb[:, k, 0, :].rearrange("p (g q) -> p g q", q=16)
        xvi = Xb[:, k, 1, :].rearrange("p (g q) -> p g q", q=16)
        tt(xr_t[:], bbr[:], akr, ALU.mult, [K(bbr), K(Ar, k)], [K(xr_t)])
        tt(tb[:], bbi[:], aki, ALU.mult, [K(bbi), K(Ai, k)], [K(tb)])
        tt(xv, xr_t[:], tb[:], ALU.subtract, [K(xr_t), K(tb)], [K(Xb, k, 0)])
        tt(xr_t[:], bbr[:], aki, ALU.mult, [K(bbr), K(Ai, k)], [K(xr_t)])
        tt(tb[:], bbi[:], akr, ALU.mult, [K(bbi), K(Ar, k)], [K(tb)])
        tt(xvi, xr_t[:], tb[:], ALU.add, [K(xr_t), K(tb)], [K(Xb, k, 1)])
    Cb = ar.alloc("Cb", [128, 2, 512], BF16)
    cp(Cb[:, 0, :].rearrange("p (g q) -> p g q", q=16), cre[:], [K(cre)], [K(Cb, 0)])
    ts(Cb[:, 1, :].rearrange("p (g q) -> p g q", q=16), cim[:], -1.0, None, ALU.mult, None, [K(cim)], [K(Cb, 1)])
    for c in range(4):
        for ri in range(2):
            for j in range(8):
                bank = (j % 2)
                mm(pb[bank][:, 0:64], Xb[0:64, 7 - j, ri, c * 128:(c + 1) * 128], identb[0:64, 0:64], True, True,
                   [K(Xb, 7 - j, ri), K(identb)], [pk(bank)])
                for e2 in range(2):
                    ts(WA[:, c, ri, j, e2 * 64:(e2 + 1) * 64], pb[bank][:, 0:64], maskE[:, e2:e2 + 1], None,
                       ALU.mult, None, [pk(bank), K(maskE)], [K(WA, c, ri, j, e2)])
    for c in range(4):
        for d in range(8):
            bank = 2 + (d % 2)
            mm(pb[bank][:, 0:128], Xb[0:64, d, 0, c * 128:(c + 1) * 128], Cb[0:64, 0, c * 128:(c + 1) * 128],
               True, False, [K(Xb, d, 0), K(Cb, 0)], [pk(bank)])
            mm(pb[bank][:, 0:128], Xb[0:64, d, 1, c * 128:(c + 1) * 128], Cb[0:64, 1, c * 128:(c + 1) * 128],
               False, True, [K(Xb, d, 1), K(Cb, 1)], [pk(bank)])
            tt(FIR[:, c, d, :], pb[bank][:, 0:128], bdmask[:], ALU.mult, [pk(bank), K(bdmask)], [K(FIR, c, d)])
    w1 = ar.alloc("w1", G3, F32); w2 = ar.alloc("w2", G3, F32)
    for jp in range(8):
        akr = bc(Ar[:, jp + 1, :]); aki = bc(Ai[:, jp + 1, :])
        tt(w1[:], cre[:], akr, ALU.mult, [K(cre), K(Ar, jp + 1)], [K(w1)])
        tt(tb[:], cim[:], aki, ALU.mult, [K(cim), K(Ai, jp + 1)], [K(tb)])
        tt(w1[:], w1[:], tb[:], ALU.subtract, [K(w1), K(tb)], [K(w1)])
        tt(w2[:], cre[:], aki, ALU.mult, [K(cre), K(Ai, jp + 1)], [K(w2)])
        tt(tb[:], cim[:], akr, ALU.mult, [K(cim), K(Ar, jp + 1)], [K(tb)])
        tt(w2[:], w2[:], tb[:], ALU.add, [K(w2), K(tb)], [K(w2)])
        for e2 in range(2):
            src1 = w1[:].rearrange("p (gp e) q -> p gp e q", e=2)[:, :, e2, :]
            src2 = w2[:].rearrange("p (gp e) q -> p gp e q", e=2)[:, :, e2, :]
            d1 = WC[:, 0, jp, :, :].rearrange("p (gp e) q -> p gp e q", e=2)[:, :, e2, :]
            d2 = WC[:, 1, jp, :, :].rearrange("p (gp e) q -> p gp e q", e=2)[:, :, e2, :]
            ts(d1, src1, maskH[:, e2:e2 + 1], None, ALU.mult, None, [K(w1), K(maskH)], [K(WC, 0, jp, e2)])
            ts(d2, src2, maskH[:, e2:e2 + 1], -1.0, ALU.mult, ALU.mult, [K(w2), K(maskH)], [K(WC, 1, jp, e2)])
    a8r = Ar[:, 8, :].rearrange("p (gp e) -> p gp e", e=2)
    a8i = Ai[:, 8, :].rearrange("p (gp e) -> p gp e", e=2)
    ts(scr[:, 0, :], a8r[:, :, 0], maskH[:, 0:1], None, ALU.mult, None, [K(Ar, 8), K(maskH)], [K(scr, 0)])
    stt(scr[:, 0, :], a8r[:, :, 1], maskH[:, 1:2], scr[:, 0, :], ALU.mult, ALU.add, [K(Ar, 8), K(maskH), K(scr, 0)], [K(scr, 0)])
    ts(sci[:, 0, :], a8i[:, :, 0], maskH[:, 0:1], None, ALU.mult, None, [K(Ai, 8), K(maskH)], [K(sci, 0)])
    stt(sci[:, 0, :], a8i[:, :, 1], maskH[:, 1:2], sci[:, 0, :], ALU.mult, ALU.add, [K(Ai, 8), K(maskH), K(sci, 0)], [K(sci, 0)])
    t16 = ar.alloc("t16", [128, 16], F32)
    for st_ in range(1, NSTEP):
        cmul(scr[:, st_, :], sci[:, st_, :], K(scr, st_), K(sci, st_),
             scr[:, st_ - 1, :], sci[:, st_ - 1, :], K(scr, st_ - 1), K(sci, st_ - 1),
             scr[:, st_ - 1, :], sci[:, st_ - 1, :], K(scr, st_ - 1), K(sci, st_ - 1), t16[:], K(t16))
    ts(scn[:], sci[:], -1.0, None, ALU.mult, None, [K(sci, i) for i in range(NSTEP)], [K(scn)])
    Rr = ar.alloc("Rr", [128, 16, 128], F32); Ri = ar.alloc("Ri", [128, 16, 128], F32)
    Rt = ar.alloc("Rt", [128, 16, 64], F32)
    memset(Rr[:, :, 127:128], 1.0, [K(Rr)])
    memset(Ri[:, :, 127:128], 0.0, [K(Ri)])
    for st_ in range(7):
        n_ = 1 << st_
        lo, hi = 128 - 2 * n_, 128 - n_
        shp = [128, 16, n_]
        arb = scr[:, st_, :].unsqueeze(2).to_broadcast(shp)
        aib = sci[:, st_, :].unsqueeze(2).to_broadcast(shp)
        srcr = Rr[:, :, hi:128]; srci = Ri[:, :, hi:128]
        tt(Rr[:, :, lo:hi], srcr, arb, ALU.mult, [K(Rr), K(scr, st_)], [K(Rr)])
        tt(Rt[:, :, 0:n_], srci, aib, ALU.mult, [K(Ri), K(sci, st_)], [K(Rt)])
        tt(Rr[:, :, lo:hi], Rr[:, :, lo:hi], Rt[:, :, 0:n_], ALU.subtract, [K(Rr), K(Rt)], [K(Rr)])
        tt(Ri[:, :, lo:hi], srci, arb, ALU.mult, [K(Ri), K(scr, st_)], [K(Ri)])
        tt(Rt[:, :, 0:n_], srcr, aib, ALU.mult, [K(Rr), K(sci, st_)], [K(Rt)])
        tt(Ri[:, :, lo:hi], Ri[:, :, lo:hi], Rt[:, :, 0:n_], ALU.add, [K(Ri), K(Rt)], [K(Ri)])
    phase_end()
    ar.release(Rt)
    for t_ in (WA, WC, FIR, scr, sci, scn):
        s.retire(t_.name)
    adaln_evac(0, 8, 7)
    stt(gm1[:], mod[:, 16:32], 1.0, n1g[:], ALU.add, ALU.mult, [K(mod, 0), K(n1g)], [K(gm1)])
    cur["slots"].free()
    ar.release(cv, sg)
    dma("sp", WA_d, WA[:].rearrange("p a b c d -> p (a b c d)"), reads=[K(WA)], writes=["WA_d"])
    dma("sp", WC_d, WC[:].rearrange("p a b c d -> p (a b c d)"), reads=[K(WC)], writes=["WC_d"])
    dma("sp", FIR_d, FIR[:].rearrange("p a b c -> p (a b c)"), reads=[K(FIR)], writes=["FIR_d"])
    if "ssmw" in dbg:
        dump_bf("WA", WA, "p a b c d -> p (a b c d)", [128, 8192], [K(WA)])
        dump_bf("FIR", FIR, "p a b c -> p (a b c)", [128, 4096], [K(FIR)])
        dump_bf("WC", WC, "p a b c d -> p (a b c d)", [128, 8192], [K(WC)])
        dump("scr", scr[:].rearrange("p a b -> p (a b)"), [128, NSTEP * 16], [K(scr)])
        dump("sci", sci[:].rearrange("p a b -> p (a b)"), [128, NSTEP * 16], [K(sci)])
    phase_end()
    ar.release(lre, lim, lst, bre, bim, cre, cim, maskE, maskH, bdmask, lr, dl_, mag, ang, t1, t2, ki, cosv, sinv,
               Ar, Ai, fr, fi, nr, den, bbr, bbi, tb, Xb, xr_t, Cb, w1, w2, t16, WA, WC, FIR)

    if stop == "pre":
        return finalize()
    amb = ar.alloc("amb", [128, 256], BF16)
    bg = ar.alloc("bg", [128, 16, 256], F32)
    bm = ar.alloc("bm", [128, 256], F32)
    am = ar.alloc("am", [128, 256], F32)
    bmb = ar.alloc("bmb", [128, 16, 256], BF16)
    dma("sp", bg[:], dram["biasg"], writes=[K(bg)])
    dma("sp", bm[:], dram["bandm"], writes=[K(bm)])
    dma("sp", am[:], dram["amask"], writes=[K(am)])
    tt(bmb[:], bg[:], bm[:].unsqueeze(1).to_broadcast([128, 16, 256]), ALU.add, [K(bg), K(bm)], [K(bmb)])
    cp(amb[:], am[:], [K(am)], [K(amb)])
    dma("sp", BM_d, bmb[:].rearrange("p a b -> p (a b)"), reads=[K(bmb)], writes=["BM_d"])
    phase_end()
    ar.release(bg, bm, am, bmb)

    if stop == "bias":
        return finalize()
    w_glu_sb = ar.alloc("wglu", [128, 4, 512], BF16)
    dma("pool", w_glu_sb[:], dram["w_glu"].rearrange("(kc p) n -> p kc n", p=128), writes=[K(w_glu_sb)])
    kprev = ar.alloc("kprev", [128, 8, 128], BF16)
    vprev = ar.alloc("vprev", [128, 512], BF16)
    carr = ar.alloc("carr", [128, 16], F32)
    cari = ar.alloc("cari", [128, 16], F32)
    memset(carr[:], 0.0, [K(carr)])
    memset(cari[:], 0.0, [K(cari)])

    def hk(h, kc):
        return [K(h, kc, 0), K(h, kc, 1)]

    def norm_h(xsrc, r0, hdst, c0, gm, gmk, sh, shk, xst, xh, junk, ss, hkey_fn):
        for t4 in range(4):
            xt = xst[t4 % 2]
            dma("sp", xt[:], xsrc[r0 + t4 * 128:r0 + (t4 + 1) * 128, :], writes=[K(xt)])
            memset(ss[:, t4:t4 + 1], 0.0, [K(ss, t4)])
            act(junk[:], xt[:], AF.Square, [K(xt), K(ss, t4)], [K(junk), K(ss, t4)], accum_out=ss[:, t4:t4 + 1])
            act(ss[:, t4:t4 + 1], ss[:, t4:t4 + 1], AF.Sqrt, [K(ss, t4), K(eps_t)], [K(ss, t4)], bias=eps_t[:], scale=1.0 / D)
            recip(ss[:, t4:t4 + 1], K(ss, t4))
            act(xh[:, t4, :], xt[:], AF.Copy, [K(xt), K(ss, t4)], [K(xh, t4)], scale=ss[:, t4:t4 + 1])
        for kc in range(16):
            bank = kc % 4
            for t4 in range(4):
                mm(pb[bank][:, t4 * 128:(t4 + 1) * 128], xh[:, t4, kc * 128:(kc + 1) * 128], identb[:], True, True,
                   [K(xh, t4), K(identb)], [pk(bank)])
            if kc % 2 == 0:
                ts(hdst[:, kc, c0:c0 + 512], pb[bank][:], gm[:, kc:kc + 1], sh[:, kc:kc + 1], ALU.mult, ALU.add,
                   [pk(bank), gmk, shk], [hkey_fn(kc)])
            else:
                act(hdst[:, kc, c0:c0 + 512], pb[bank][:], AF.Identity, [pk(bank), gmk, shk], [hkey_fn(kc)],
                    bias=sh[:, kc:kc + 1], scale=gm[:, kc:kc + 1])

    def mixer_front(xsrc, base):
        xst = [ar.alloc("xst%d" % i, [128, D], F32) for i in range(2)]
        xh = ar.alloc("xh", [128, 4, D], BF16)
        junk = ar.alloc("junk", [128, D], BF16)
        ss = ar.alloc("ss", [128, 4], F32)
        h = ar.alloc("h", [128, 16, ST], BF16)
        for half in range(2):
            norm_h(xsrc, base + half * 512, h, half * 512, gm1, K(gm1), sh1, K(mod, 0), xst, xh, junk, ss,
                   lambda kc, half=half: K(h, kc, half))
        ar.release(xst[0], xst[1], xh, junk, ss)
        return h

    def proj_u(h, u, pre):
        sl = next_slot()
        dma("pool", sl[:], w_in_v[:, :, 1536:2048], writes=SK(sl))
        for mc in range(4):
            for half in range(2):
                bank = (mc * 2 + half) % 4
                for kc in range(16):
                    mm(pb[bank][:], sl[:, kc, mc * 128:(mc + 1) * 128], h[:, kc, half * 512:(half + 1) * 512],
                       kc == 0, kc == 15, [K(sl, mc), K(h, kc, half)], [pk(bank)])
                if pre:
                    act(u[:, mc, half * 512:(half + 1) * 512], pb[bank][:], AF.Identity, [pk(bank), K(buf_), K(flag)],
                        [K(u, mc, half)], bias=buf_[:, mc:mc + 1], scale=flag[:, 0:1])
                else:
                    act(u[:, mc, half * 512:(half + 1) * 512], pb[bank][:], AF.Identity, [pk(bank), K(b_u)],
                        [K(u, mc, half)], bias=b_u[:, mc:mc + 1], scale=1.0)

    def proj_k(h, col_lo, ncols, dst_fn, dkey_fn):
        for half8 in range(2):
            sl = next_slot()
            memset(sl[:], 0.0, SK(sl), eng="pool")
            for cc in range(4):
                ch = half8 * 4 + cc
                g, e2 = ch // 2, ch % 2
                c0 = cc * 128 + e2 * 64
                dma("pool", sl[:, :, c0:c0 + 64], w_in_v[:, :, 1024 + g * 64:1024 + (g + 1) * 64], writes=[K(sl, cc)])
            for cc in range(4):
                ch = half8 * 4 + cc
                bank = 4 + (cc % 2)
                for kc in range(16):
                    mm(pb[bank][:, 0:ncols], sl[:, kc, cc * 128:(cc + 1) * 128], h[:, kc, col_lo:col_lo + ncols],
                       kc == 0, kc == 15, [K(sl, cc)] + hk(h, kc), [pk(bank)])
                act(dst_fn(ch), pb[bank][:, 0:ncols], AF.Identity, [pk(bank), K(b_k)], [dkey_fn(ch)],
                    bias=b_k[:, ch:ch + 1], scale=1.0)

    def proj_v(h, col_lo, nblk, dst_fn, dkey_fn):
        sl = next_slot()
        for g in range(4):
            for e2 in range(2):
                c0 = g * 128 + e2 * 64
                dma("pool", sl[:, :, c0:c0 + 64], w_in_v[:, :, 1280 + g * 64:1280 + (g + 1) * 64], writes=[K(sl, g)])
        for nb in range(nblk):
            bank = 6 + (nb % 2)
            for kc in range(16):
                mm(pb[bank][:], h[:, kc, col_lo + nb * 128:col_lo + (nb + 1) * 128], sl[:, kc, :], kc == 0, kc == 15,
                   SK(sl) + hk(h, kc), [pk(bank)])
            tt(dst_fn(nb), pb[bank][:], bvrow[:], ALU.add, [pk(bank), K(bvrow)], [dkey_fn(nb)])

    def alloc_Z():
        Za = {"r": ar.alloc("Zar", [128, 16, 129], F32), "i": ar.alloc("Zai", [128, 16, 129], F32)}
        Zb = {"r": ar.alloc("Zbr", [128, 16, 129], F32), "i": ar.alloc("Zbi", [128, 16, 129], F32)}
        return Za, Zb

    def free_Z(Za, Zb):
        ar.release(Za["r"], Za["i"], Zb["r"], Zb["i"])

    def ssm_stageA_scan(u, WAt, Za, Zb, stop_=None, reduce_only=False):
        cp(Za["r"][:, :, 0], carr[:], [K(carr)], [K(Za["r"], "c")])
        cp(Za["i"][:, :, 0], cari[:], [K(cari)], [K(Za["i"], "c")])
        if stop_ == "Aa":
            return Za
        umc = [ar.alloc("umc%d" % i, [128, 4, 8, 128], BF16) for i in range(2)]
        for c in range(4):
            if stop_ == "Ab" and c == 1:
                return Za
            um = umc[c % 2]
            usrc = u[:, c, :].rearrange("p (i j) -> p j i", j=8)
            for r in range(4):
                ts(um[:, r, :, :], usrc, maskP[:, r:r + 1], None, ALU.mult, None,
                   [K(u, c, 0), K(u, c, 1), K(maskP)], [K(um, r)])
            if stop_ == "Ac":
                return Za
            for r in range(4):
                pair = c * 4 + r
                b0_ = 4 + (pair % 2) * 2
                for ri in range(2):
                    for j in range(8):
                        mm(pb[b0_ + ri][:, 0:128], WAt[:, c, ri, j, :], um[:, r, j, :], j == 0, j == 7,
                           [K(WAt), K(um, r)], [pk(b0_ + ri)])
                cp(Za["r"][:, pair, 1:129], pb[b0_][:, 0:128], [pk(b0_)], [K(Za["r"], pair)])
                act(Za["i"][:, pair, 1:129], pb[b0_ + 1][:, 0:128], AF.Copy, [pk(b0_ + 1)], [K(Za["i"], pair)])
        retire_psum()
        for p_ in ("r", "i"):
            s.retire(Za[p_].name)
        if stop_ is not None and stop_.startswith("A"):
            return Za
        if reduce_only:
            Er = Za["r"][:, :, 1:129]; Ei = Za["i"][:, :, 1:129]
            P1 = Zb["r"][:, :, 0:128]; P2 = Zb["i"][:, :, 0:128]
            red = ar.alloc("red", [128, 4, 16], F32)
            tt(P1, Er, Rr[:], ALU.mult, [K(Za["r"]), K(Rr)], [K(Zb["r"])])
            s.add("dve", lambda e: e.reduce_sum(out=red[:, 0, :], in_=P1, axis=AX.X), reads=[K(Zb["r"])], writes=[K(red, 0)])
            tt(P2, Ei, Ri[:], ALU.mult, [K(Za["i"]), K(Ri)], [K(Zb["i"])])
            s.add("dve", lambda e: e.reduce_sum(out=red[:, 1, :], in_=P2, axis=AX.X), reads=[K(Zb["i"])], writes=[K(red, 1)])
            tt(P1, Er, Ri[:], ALU.mult, [K(Za["r"]), K(Ri), K(red, 0)], [K(Zb["r"])])
            s.add("dve", lambda e: e.reduce_sum(out=red[:, 2, :], in_=P1, axis=AX.X), reads=[K(Zb["r"])], writes=[K(red, 2)])
            tt(P2, Ei, Rr[:], ALU.mult, [K(Za["i"]), K(Rr), K(red, 1)], [K(Zb["i"])])
            s.add("dve", lambda e: e.reduce_sum(out=red[:, 3, :], in_=P2, axis=AX.X), reads=[K(Zb["i"])], writes=[K(red, 3)])
            a128r = scr[:, 7, :]; a128i = sci[:, 7, :]
            nr_ = ar.alloc("nr_", [128, 2, 16], F32)
            tt(red[:, 0, :], red[:, 0, :], red[:, 1, :], ALU.subtract, [K(red, 0), K(red, 1)], [K(red, 0)])
            tt(red[:, 2, :], red[:, 2, :], red[:, 3, :], ALU.add, [K(red, 2), K(red, 3)], [K(red, 2)])
            tt(nr_[:, 0, :], carr[:], a128r, ALU.mult, [K(carr), K(scr)], [K(nr_, 0)])
            tt(red[:, 1, :], cari[:], a128i, ALU.mult, [K(cari), K(sci), K(red, 0)], [K(red, 1)])
            tt(nr_[:, 0, :], nr_[:, 0, :], red[:, 1, :], ALU.subtract, [K(nr_, 0), K(red, 1)], [K(nr_, 0)])
            tt(nr_[:, 1, :], cari[:], a128r, ALU.mult, [K(cari), K(scr)], [K(nr_, 1)])
            tt(red[:, 3, :], carr[:], a128i, ALU.mult, [K(carr), K(sci), K(red, 2)], [K(red, 3)])
            tt(nr_[:, 1, :], nr_[:, 1, :], red[:, 3, :], ALU.add, [K(nr_, 1), K(red, 3)], [K(nr_, 1)])
            tt(carr[:], nr_[:, 0, :], red[:, 0, :], ALU.add, [K(nr_, 0), K(red, 0), K(nr_, 1)], [K(carr)])
            tt(cari[:], nr_[:, 1, :], red[:, 2, :], ALU.add, [K(nr_, 1), K(red, 2)], [K(cari)])
            ar.release(*umc, red, nr_)
            return Za
        src, dst = Za, Zb
        for st_ in range(NSTEP):
            sh_ = 1 << st_
            L = 129 - sh_
            cp(dst["r"][:, :, 0:sh_], src["r"][:, :, 0:sh_], [K(src["r"])], [K(dst["r"], "h")])
            cp(dst["i"][:, :, 0:sh_], src["i"][:, :, 0:sh_], [K(src["i"])], [K(dst["i"], "h")])
            for pair in range(16):
                ar_ = scr[:, st_, pair:pair + 1]
                stt(dst["r"][:, pair, sh_:129], src["r"][:, pair, 0:L], ar_, src["r"][:, pair, sh_:129], ALU.mult, ALU.add,
                    [K(src["r"]), K(scr)], [K(dst["r"], pair)])
                stt(dst["i"][:, pair, sh_:129], src["i"][:, pair, 0:L], ar_, src["i"][:, pair, sh_:129], ALU.mult, ALU.add,
                    [K(src["i"]), K(scr)], [K(dst["i"], pair)])
            for pair in range(16):
                ai_ = sci[:, st_, pair:pair + 1]; an_ = scn[:, st_, pair:pair + 1]
                stt(dst["r"][:, pair, sh_:129], src["i"][:, pair, 0:L], an_, dst["r"][:, pair, sh_:129], ALU.mult, ALU.add,
                    [K(src["i"]), K(scn), K(dst["r"], pair)], [K(dst["r"], pair)])
                stt(dst["i"][:, pair, sh_:129], src["r"][:, pair, 0:L], ai_, dst["i"][:, pair, sh_:129], ALU.mult, ALU.add,
                    [K(src["r"]), K(sci), K(dst["i"], pair)], [K(dst["i"], pair)])
            for p_ in ("r", "i"):
                s.retire(src[p_].name)
                s.retire(dst[p_].name)
            src, dst = dst, src
        cp(carr[:], src["r"][:, :, 128], [K(src["r"])], [K(carr)])
        cp(cari[:], src["i"][:, :, 128], [K(src["i"])], [K(cari)])
        ar.release(*umc)
        return src

    def load_WA():
        WAt = ar.alloc("WAt", [128, 4, 2, 8, 128], BF16)
        dma("sp", WAt[:].rearrange("p a b c d -> p (a b c d)"), WA_d, reads=["WA_d"], writes=[K(WAt)])
        return WAt

    for pt in range(2):
        cur["slots"] = Slots(2)
        h = mixer_front(dram["xp"], pt * ST)
        if stop == "h%d" % pt:
            if "proj" in dbg:
                for kc in range(16):
                    s.retire(h.name)
                dump_bf("h", h, "p a b -> p (a b)", [128, 16 * ST], [K(h)])
            return finalize()
        u = ar.alloc("u", [128, 4, ST], BF16)
        proj_u(h, u, True)
        if stop == "u%d" % pt:
            return finalize()
        if pt == 1:
            proj_k(h, ST - 128, 128, lambda ch: kprev[:, ch, :], lambda ch: K(kprev, ch))
            if stop == "k1":
                return finalize()
            proj_v(h, ST - 128, 1, lambda nb: vprev[:], lambda nb: K(vprev))
            if stop == "v1":
                return finalize()
        phase_end()
        cur["slots"].free()
        ar.release(h)
        WAt = load_WA()
        Za, Zb = alloc_Z()
        ssm_stageA_scan(u, WAt, Za, Zb, stop, reduce_only=True)
        if stop in ("A%d" % pt, "s%d" % pt, "Aa", "Ab", "Ac"):
            return finalize()
        phase_end()
        free_Z(Za, Zb)
        ar.release(u, WAt)
    s.retire(kprev.name)
    ar.release(Rr, Ri)
    if "carry" in dbg:
        dump("carr", carr[:], [128, 16], [K(carr)])
        dump("cari", cari[:], [128, 16], [K(cari)])
        dump_bf("kprev", kprev, "p a b -> p (a b)", [128, 1024], [K(kprev)])
        dump_bf("vprev", vprev, None, [128, 512], [K(vprev)])

    if stop == "carry":
        return finalize()
    for st in range(TOK // ST):
        base = st * ST
        d0 = (st == 0)
        sgA = ar.alloc("sgA", [128, 16, ST], BF16, top=True)
        sgS = ar.alloc("sgS", [128, 16, ST], BF16, top=True)
        cur["slots"] = Slots(2)
        h = mixer_front(dram["x"], base)
        qb = ar.alloc("qb", [128, 8, ST], BF16)
        kb = ar.alloc("kb", [128, 8, 128 + ST], BF16)
        vT = ar.alloc("vT", [128, 9, 512], BF16)
        u = ar.alloc("u", [128, 4, ST], BF16)
        cp(kb[:, :, 0:128], kprev[:], [K(kprev)], [K(kb, "p")])
        cp(vT[:, 0, :], vprev[:], [K(vprev)], [K(vT, 0)])
        for blk in range(2):
            sl = next_slot()
            dma("pool", sl[:], w_in_v[:, :, blk * 512:(blk + 1) * 512], writes=SK(sl))
            for mc in range(4):
                ch = blk * 4 + mc
                for half in range(2):
                    bank = (mc * 2 + half) % 4
                    for kc in range(16):
                        mm(pb[bank][:], sl[:, kc, mc * 128:(mc + 1) * 128], h[:, kc, half * 512:(half + 1) * 512],
                           kc == 0, kc == 15, [K(sl, mc), K(h, kc, half)], [pk(bank)])
                    act(qb[:, ch, half * 512:(half + 1) * 512], pb[bank][:], AF.Identity, [pk(bank), K(bq8)],
                        [K(qb, ch, half)], bias=bq8[:, ch:ch + 1], scale=0.125)
        for half in range(2):
            proj_k(h, half * 512, 512, lambda ch, half=half: kb[:, ch, 128 + half * 512:128 + (half + 1) * 512],
                   lambda ch, half=half: K(kb, ch, half))
        proj_v(h, 0, 8, lambda nb: vT[:, 1 + nb, :], lambda nb: K(vT, 1 + nb))
        proj_u(h, u, False)
        phase_end()
        cur["slots"].free()
        for t_ in (qb, kb, vT, u, h):
            s.retire(t_.name)
        if "proj" in dbg and d0:
            dump_bf("h", h, "p a b -> p (a b)", [128, 16 * ST], [K(h)])
            dump_bf("q", qb, "p a b -> p (a b)", [128, 8 * ST], [K(qb)])
            dump_bf("k", kb, "p a b -> p (a b)", [128, 8 * (128 + ST)], [K(kb)])
            dump_bf("v", vT, "p a b -> p (a b)", [128, 9 * 512], [K(vT)])
            dump_bf("u", u, "p a b -> p (a b)", [128, 4 * ST], [K(u)])

        biasm = ar.alloc("biasm", [128, 16, 256], BF16)
        dma("sp", biasm[:].rearrange("p a b -> p (a b)"), BM_d, reads=["BM_d"], writes=[K(biasm)])
        ptile = [ar.alloc("ptile%d" % i, [128, 256], BF16) for i in range(4)]
        pTt = [ar.alloc("pTt%d" % i, [128, 2, 128], BF16) for i in range(4)]
        dgt = [ar.alloc("dgt%d" % i, [128, 128], BF16) for i in range(4)]
        stat = [ar.alloc("stat%d" % i, [128, 4], F32) for i in range(4)]
        groups = [(nb, hg) for nb in range(8) for hg in range(8)]

        def attn_A(gi):
            nb, hg = groups[gi]
            first = (st == 0 and nb == 0)
            H = []
            for u_ in range(2):
                hd = hg * 2 + u_
                i4 = (gi % 2) * 2 + u_
                H.append(dict(hd=hd, i4=i4, bank=i4, ps=pb[i4][:, 0:256], kvh=hd // 4, ch=hd // 2, e2=hd % 2, sv=stat[i4]))
            for c_ in H:
                mm(c_["ps"], qb[:, c_["ch"], nb * 128:(nb + 1) * 128],
                   kb[:, c_["kvh"] * 2 + c_["e2"], nb * 128:nb * 128 + 256], True, False,
                   [K(qb, c_["hd"], nb), K(kb)], [pk(c_["bank"])])
                mm(c_["ps"], identb[:], biasm[:, c_["hd"], :], False, not first, [K(identb), K(biasm)], [pk(c_["bank"])])
                if first:
                    mm(c_["ps"], identb[:], amb[:], False, True, [K(identb), K(amb)], [pk(c_["bank"])])
            for c_ in H:
                sv = c_["sv"]
                s.add("dve", lambda e, sv=sv, ps=c_["ps"]: e.reduce_max(out=sv[:, 0:1], in_=ps, axis=AX.X),
                      reads=[pk(c_["bank"])], writes=[K(sv, 0)])
            for c_ in H:
                sv = c_["sv"]; hd = c_["hd"]
                ts(sv[:, 0:1], sv[:, 0:1], sinks[:, hd:hd + 1], -1.0, ALU.max, ALU.mult, [K(sv, 0), K(sinks)], [K(sv, 0)])
            for c_ in H:
                sv = c_["sv"]; i4 = c_["i4"]
                act(ptile[i4][:], c_["ps"], AF.Exp, [pk(c_["bank"]), K(sv, 0), K(sv, 1)], [K(ptile[i4]), K(sv, 1)],
                    bias=sv[:, 0:1], scale=1.0, accum_out=sv[:, 1:2])
            for c_ in H:
                sv = c_["sv"]; hd = c_["hd"]
                act(sv[:, 2:3], sinks[:, hd:hd + 1], AF.Exp, [K(sinks), K(sv, 0)], [K(sv, 2)], bias=sv[:, 0:1], scale=1.0)
            for c_ in H:
                sv = c_["sv"]
                tt(sv[:, 1:2], sv[:, 1:2], sv[:, 2:3], ALU.add, [K(sv, 1), K(sv, 2)], [K(sv, 1)])
            for c_ in H:
                sv = c_["sv"]
                recip(sv[:, 1:2], K(sv, 1))
            for k_, c_ in enumerate(H):
                sv = c_["sv"]; i4 = c_["i4"]
                ts(dgt[i4][:], identf[:], sv[:, 1:2], None, ALU.mult, None, [K(identf), K(sv, 1)], [K(dgt[i4])],
                   eng=("pool" if k_ == 1 else "dve"))

        def attn_B(gi):
            for u_ in range(2):
                i4 = (gi % 2) * 2 + u_
                bank, sub = 4, 0
                c0_ = u_ * 256
                for jk in range(2):
                    mm(pb[bank][:, c0_ + jk * 128:c0_ + (jk + 1) * 128], ptile[i4][:, jk * 128:(jk + 1) * 128],
                       dgt[i4][:], True, True, [K(ptile[i4]), K(dgt[i4])], [pk(bank, sub)])
                cp(pTt[i4][:].rearrange("p a b -> p (a b)"), pb[bank][:, c0_:c0_ + 256], [pk(bank, sub)], [K(pTt[i4])])

        def attn_C(gi):
            nb, hg = groups[gi]
            for u_ in range(2):
                hd = hg * 2 + u_
                i4 = (gi % 2) * 2 + u_
                kvh = hd // 4
                ch = hd // 2
                e2 = hd % 2
                bk_ = 5
                ps = pb[bk_][:, u_ * 128:(u_ + 1) * 128]
                for jk in range(2):
                    mm(ps, vT[:, nb + jk, kvh * 128:(kvh + 1) * 128], pTt[i4][:, jk, :], jk == 0, jk == 1,
                       [K(vT), K(pTt[i4])], [pk(bk_)])
                act(qb[e2 * 64:(e2 + 1) * 64, ch, nb * 128:(nb + 1) * 128],
                    pb[bk_][e2 * 64:(e2 + 1) * 64, u_ * 128:(u_ + 1) * 128], AF.Copy, [pk(bk_)], [K(qb, hd, nb)])

        gw = [ar.alloc("gw%d" % i, [128, 16, 256], BF16) for i in range(3)]
        units = []
        bdma = []
        done = []
        pending = []

        def gate_block(which, mb2):
            k = len(bdma)
            sl = gw[k % 3]
            col0 = (2048 if which == 0 else 4096) + mb2 * 256
            sg_t = sgA if which == 0 else sgS
            bias_t = b_ga if which == 0 else b_gs
            bdma.append(lambda: dma("pool", sl[:], w_in_v[:, :, col0:col0 + 256], writes=[K(sl)]))

            def unit(mc, half):
                m = mb2 * 2 + mc
                bank = 6 + (len(done) % 2)
                done.append(1)
                cols = slice(half * 512, (half + 1) * 512)
                for kc in range(16):
                    mm(pb[bank][:], sl[:, kc, mc * 128:(mc + 1) * 128], h[:, kc, cols], kc == 0, kc == 15,
                       [K(sl), K(h)], [pk(bank)])
                pending.append(lambda: ts(sg_t[:, m, cols], pb[bank][:], bias_t[:, m:m + 1], None, ALU.add, None,
                                          [pk(bank), K(bias_t)], [K(sg_t, m, half)]))
            for mc in range(2):
                for half in range(2):
                    units.append((k, lambda mc=mc, half=half: unit(mc, half)))

        def adaln_half(jb2):
            k = len(bdma)
            sl = gw[k % 3]
            bdma.append(lambda: dma("pool", sl[:], wav[:, :, jb2 * 256:(jb2 + 1) * 256], writes=[K(sl)]))

            def unit(mc):
                j = jb2 * 2 + mc
                for kc in range(16):
                    mm(pb[5][:, 384 + (j - 32):385 + (j - 32)], sl[:, kc, mc * 128:(mc + 1) * 128], cs[:, kc:kc + 1],
                       kc == 0, kc == 15, [K(sl), K(cs)], [pk(5)])
            for mc in range(2):
                units.append((k, lambda mc=mc: unit(mc)))

        blocks = [(0, i) for i in range(8)] + [(1, i) for i in range(8)]
        for (w_, mb2) in blocks:
            gate_block(w_, mb2)
        issued = {"n": 0}

        def emit_unit():
            k, fn = units.pop(0)
            while issued["n"] < min(len(bdma), k + 3):
                bdma[issued["n"]]()
                issued["n"] += 1
            fn()

        NG = len(groups)
        per = (len(units) + NG - 1) // NG
        for gi in range(NG + 2):
            if gi < NG:
                attn_A(gi)
            prev_pending = list(pending)
            del pending[:]
            for f_ in prev_pending:
                f_()
            for _ in range(per):
                if units:
                    emit_unit()
            if 0 <= gi - 1 < NG:
                attn_B(gi - 1)
            if 0 <= gi - 2 < NG:
                attn_C(gi - 2)
        while units or pending:
            prev_pending = list(pending)
            del pending[:]
            for f_ in prev_pending:
                f_()
            if units:
                emit_unit()
        cp(kprev[:], kb[:, :, ST:ST + 128], [K(kb)], [K(kprev)])
        cp(vprev[:], vT[:, 8, :], [K(vT)], [K(vprev)])
        phase_end()
        ar.release(*ptile, *pTt, *dgt, *stat, *gw)
        ar.release(kb, vT, biasm, h)
        s.retire(qb.name)
        s.retire(sgA.name)
        s.retire(sgS.name)
        if "attn" in dbg and d0:
            dump_bf("attn", qb, "p a b -> p (a b)", [128, 8 * ST], [K(qb)])

        WAt = load_WA()
        Za, Zb = alloc_Z()
        Zf = ssm_stageA_scan(u, WAt, Za, Zb)
        if st == 0:
            cur["slots"] = Slots(2)
            adaln_blocks(8, 24, 3)
            s.retire(mod.name)
            stt(gm2[:], mod[:, 64:80], 1.0, n2g[:], ALU.add, ALU.mult, [K(mod), K(n2g)], [K(gm2)])
        Sb = {"r": ar.alloc("Sbr", [128, 16, 128], BF16), "i": ar.alloc("Sbi", [128, 16, 128], BF16)}
        cp(Sb["r"][:], Zf["r"][:, :, 0:128], [K(Zf["r"])], [K(Sb["r"])])
        act(Sb["i"][:], Zf["i"][:, :, 0:128], AF.Copy, [K(Zf["i"])], [K(Sb["i"])])
        if "ssm" in dbg and d0:
            dump("Zr", Zf["r"][:].rearrange("p a b -> p (a b)"), [128, 16 * 129], [K(Zf["r"])])
            dump("Zi", Zf["i"][:].rearrange("p a b -> p (a b)"), [128, 16 * 129], [K(Zf["i"])])
        phase_end()
        free_Z(Za, Zb)
        ar.release(WAt)
        if st == 0:
            cur["slots"].free()
            ar.release(cs, bada)
        WCt = ar.alloc("WCt", [128, 2, 8, 32, 16], BF16)
        FIRt = ar.alloc("FIRt", [128, 4, 8, 128], BF16)
        dma("sp", WCt[:].rearrange("p a b c d -> p (a b c d)"), WC_d, reads=["WC_d"], writes=[K(WCt)])
        dma("sp", FIRt[:].rearrange("p a b c -> p (a b c)"), FIR_d, reads=["FIR_d"], writes=[K(FIRt)])
        zf = ar.alloc("zf", [128, 4, ST], BF16)
        zg = ar.alloc("zg", [128, 4, ST], BF16)
        yt = [ar.alloc("yt%d" % i, [128, 512], F32) for i in range(2)]
        y2 = [ar.alloc("y2%d" % i, [128, 512], F32) for i in range(2)]
        udt = [ar.alloc("udt%d" % i, [128, 8, 64], BF16) for i in range(2)]
        ydbg = dbg_out("y", [128, 4, ST]) if ("ssm" in dbg and d0) else None
        for c in range(4):
            for half in range(2):
                pv = pb[0][:].rearrange("p (j i) -> p j i", j=8)
                udn = udt[half]
                unat = u[:, c, half * 512:(half + 1) * 512].rearrange("p (i j) -> p j i", j=8)
                cp(udn[:], unat, [K(u)], [K(udn)])
                for d in range(8):
                    mm(pv[:, d:8, :], FIRt[:, c, d, :], udn[:, 0:8 - d, :], d == 0, d == 7, [K(FIRt), K(udn)], [pk(0)], skip=True)
                for r in range(4):
                    pair = c * 4 + r
                    pvr = pb[1 + r][:].rearrange("p (j i) -> p j i", j=8)
                    for jp in range(8):
                        for ri in range(2):
                            mm(pvr[:, jp, :], WCt[:, ri, jp, c * 8:(c + 1) * 8, :].rearrange("p a b -> p (a b)"),
                               Sb["r" if ri == 0 else "i"][:, pair, half * 64:(half + 1) * 64], ri == 0, ri == 1,
                               [K(WCt), K(Sb["r"]), K(Sb["i"])], [pk(1 + r)], skip=True)
                y = yt[half]
                yv = y[:].rearrange("p (i j) -> p j i", j=8)
                stt(yv, udn[:], dsk[:, c:c + 1], pv, ALU.mult, ALU.add, [K(udn), K(dsk), pk(0)], [K(y)])
                for r in range(4):
                    stt(yv, pb[1 + r][:].rearrange("p (j i) -> p j i", j=8), maskP[:, r:r + 1], yv, ALU.mult, ALU.add,
                        [pk(1 + r), K(maskP), K(y)], [K(y)])
                if ydbg is not None:
                    out_ops.append(dma("sp", ydbg[:, c, half * 512:(half + 1) * 512], y[:], reads=[K(y)]))
                t_ = y2[half]
                tt(t_[:], y[:], y[:], ALU.mult, [K(y)], [K(t_)])
                ts(t_[:], t_[:], 0.044715, 1.0, ALU.mult, ALU.add, [K(t_)], [K(t_)])
                tt(t_[:], t_[:], y[:], ALU.mult, [K(t_), K(y)], [K(t_)])
                act(t_[:], t_[:], AF.Sigmoid, [K(t_)], [K(t_)], scale=1.5957691216057308)
                tt(zg[:, c, half * 512:(half + 1) * 512], y[:], t_[:], ALU.mult, [K(y), K(t_)], [K(zg, c, half)])
        for mc in range(4):
            for half in range(2):
                bank = 5 + (mc * 2 + half) % 3
                for kc in range(4):
                    mm(pb[bank][:], w_glu_sb[:, kc, mc * 128:(mc + 1) * 128], zg[:, kc, half * 512:(half + 1) * 512],
                       kc == 0, kc == 3, [K(w_glu_sb), K(zg, kc, half)], [pk(bank)])
                t_ = y2[half]
                act(t_[:], pb[bank][:], AF.Sigmoid, [pk(bank), K(b_glu)], [K(t_)], bias=b_glu[:, mc:mc + 1], scale=1.0)
                tt(zf[:, mc, half * 512:(half + 1) * 512], zg[:, mc, half * 512:(half + 1) * 512], t_[:], ALU.mult,
                   [K(zg, mc, half), K(t_)], [K(zf, mc, half)])
        phase_end()
        ar.release(Sb["r"], Sb["i"], zg, *yt, *y2, *udt, u, WCt, FIRt)
        s.retire(zf.name)
        if "ssm" in dbg and d0:
            dump_bf("zf", zf, "p a b -> p (a b)", [128, 4 * ST], [K(zf)])

        mg = ar.alloc("mg", [128, 16, ST], BF16, top=True)
        sA = [ar.alloc("sA%d" % i, [128, 512], F32) for i in range(2)]
        sS = [ar.alloc("sS%d" % i, [128, 512], F32) for i in range(2)]
        wsets = [ar.alloc("ws3", [128, 12, 256], BF16) for _ in range(3)]
        wap_v = wview(dram["w_ap"])
        wsp_v = wview(dram["w_sp"])
        it = 0
        for mb in range(8):
            s3 = wsets[mb % 3]
            c_lo, c_hi = mb * 256, (mb + 1) * 256
            dma("pool", s3[:, 0:8, :], wap_v[:, :, c_lo:c_hi], writes=[K(s3, "a")])
            dma("pool", s3[:, 8:12, :], wsp_v[:, :, c_lo:c_hi], writes=[K(s3, "s")])
            for mc in range(2):
                m = mb * 2 + mc
                for half in range(2):
                    b0 = (it % 4) * 2
                    it += 1
                    cols = slice(half * 512, (half + 1) * 512)
                    for kc in range(8):
                        mm(pb[b0][:], s3[:, kc, mc * 128:(mc + 1) * 128], qb[:, kc, cols], kc == 0, kc == 7,
                           [K(s3, "a"), K(qb)], [pk(b0)])
                    for kc in range(4):
                        mm(pb[b0 + 1][:], s3[:, 8 + kc, mc * 128:(mc + 1) * 128], zf[:, kc, cols], kc == 0, kc == 3,
                           [K(s3, "s"), K(zf)], [pk(b0 + 1)])
                    a_ = sA[it % 2]; g_ = sS[it % 2]
                    act(a_[:], sgA[:, m, cols], AF.Sigmoid, [K(sgA)], [K(a_)])
                    act(g_[:], sgS[:, m, cols], AF.Sigmoid, [K(sgS)], [K(g_)])
                    tt(a_[:], a_[:], pb[b0][:], ALU.mult, [K(a_), pk(b0)], [K(a_)])
                    tt(g_[:], g_[:], pb[b0 + 1][:], ALU.mult, [K(g_), pk(b0 + 1)], [K(g_)])
                    tt(mg[:, m, cols], a_[:], g_[:], ALU.add, [K(a_), K(g_)], [K(mg, m, half)])
        phase_end()
        ar.release(*wsets)
        ar.release(*sA, *sS, qb, zf, sgA, sgS)
        s.retire(mg.name)
        if "merge" in dbg and d0:
            dump_bf("merged", mg, "p a b -> p (a b)", [128, 16 * ST], [K(mg)])

        cur["slots"] = Slots(2)
        acc = ar.alloc("acc", [128, 8, D], F32)
        rowb = ar.alloc("rowb", [128, D], F32)
        tmpw = [ar.alloc("tmpw%d" % i, [128, 512], F32) for i in range(2)]
        make_row(rowb, mod[:, 32:48], K(mod))
        xv = dram["x"][base:base + ST, :].rearrange("(t p) f -> p t f", p=128)
        for t8 in range(8):
            dma("sp", acc[:, t8, :], xv[:, t8, :], writes=[K(acc, t8, cb) for cb in range(4)])
        retire_psum()
        wo_v = wview(dram["w_out"])
        it = 0
        for cb in range(4):
            sl = next_slot()
            dma("pool", sl[:], wo_v[:, :, cb * 512:(cb + 1) * 512], writes=SK(sl))
            for t8 in range(8):
                bank = it % 8
                it += 1
                for kc in range(16):
                    mm(pb[bank][:], mg[:, kc, t8 * 128:(t8 + 1) * 128], sl[:, kc, :], kc == 0, kc == 15,
                       SK(sl) + [K(mg)], [pk(bank)])
                tw = tmpw[it % 2]
                tt(tw[:], pb[bank][:], rowb[:, cb * 512:(cb + 1) * 512], ALU.mult, [pk(bank), K(rowb)], [K(tw)])
                tt(acc[:, t8, cb * 512:(cb + 1) * 512], acc[:, t8, cb * 512:(cb + 1) * 512], tw[:], ALU.add,
                   [K(acc, t8, cb), K(tw)], [K(acc, t8, cb)])
        if "x1" in dbg and d0:
            d_ = dbg_out("x1", [ST, D])
            for t8 in range(8):
                out_ops.append(dma("sp", d_[t8 * 128:(t8 + 1) * 128, :], acc[:, t8, :],
                                   reads=[K(acc, t8, cb) for cb in range(4)]))
        phase_end()
        cur["slots"].free()
        s.retire(mg.name)
        xh = ar.alloc("xh2", [128, 4, D], BF16)
        junk = ar.alloc("junk2", [128, D], BF16)
        ss = ar.alloc("ss2", [128, 8], F32)
        for half in range(2):
            for t4 in range(4):
                t8 = half * 4 + t4
                ak = [K(acc, t8, cb) for cb in range(4)]
                memset(ss[:, t8:t8 + 1], 0.0, [K(ss, t8)])
                act(junk[:], acc[:, t8, :], AF.Square, ak + [K(ss, t8)], [K(junk), K(ss, t8)], accum_out=ss[:, t8:t8 + 1])
                act(ss[:, t8:t8 + 1], ss[:, t8:t8 + 1], AF.Sqrt, [K(ss, t8), K(eps_t)], [K(ss, t8)], bias=eps_t[:], scale=1.0 / D)
                recip(ss[:, t8:t8 + 1], K(ss, t8))
                act(xh[:, t4, :], acc[:, t8, :], AF.Copy, ak + [K(ss, t8)], [K(xh, t4)], scale=ss[:, t8:t8 + 1])
            for kc in range(16):
                bank = kc % 4
                for t4 in range(4):
                    mm(pb[bank][:, t4 * 128:(t4 + 1) * 128], xh[:, t4, kc * 128:(kc + 1) * 128], identb[:], True, True,
                       [K(xh, t4), K(identb)], [pk(bank)])
                if kc % 2 == 0:
                    ts(mg[:, kc, half * 512:(half + 1) * 512], pb[bank][:], gm2[:, kc:kc + 1], sh2[:, kc:kc + 1],
                       ALU.mult, ALU.add, [pk(bank), K(gm2), K(mod)], [K(mg, kc, half)])
                else:
                    act(mg[:, kc, half * 512:(half + 1) * 512], pb[bank][:], AF.Identity, [pk(bank), K(gm2), K(mod)],
                        [K(mg, kc, half)], bias=sh2[:, kc:kc + 1], scale=gm2[:, kc:kc + 1])
        make_row(rowb, mod[:, 80:96], K(mod))
        phase_end()
        ar.release(xh, junk, ss)
        s.retire(mg.name)
        h2 = mg

        cur["slots"] = Slots(2)
        hid = ar.alloc("hid", [128, 16, ST], BF16)
        rl = [ar.alloc("rl%d" % i, [128, 512], F32) for i in range(2)]
        w1_v = wview(dram["w_ff1"])
        w2_v = dram["w_ff2"].rearrange("(g kc p) n -> p g kc n", p=128, kc=16)
        it = 0
        it2 = 0
        for g in range(4):
            for hb in range(4):
                sl = next_slot()
                dma("pool", sl[:], w1_v[:, :, g * 2048 + hb * 512:g * 2048 + (hb + 1) * 512], writes=SK(sl))
                for mc in range(4):
                    hc = hb * 4 + mc
                    for half in range(2):
                        bank = it % 4
                        it += 1
                        for kc in range(16):
                            mm(pb[bank][:], sl[:, kc, mc * 128:(mc + 1) * 128], h2[:, kc, half * 512:(half + 1) * 512],
                               kc == 0, kc == 15, [K(sl, mc), K(h2)], [pk(bank)])
                        r_ = rl[it % 2]
                        act(r_[:], pb[bank][:], AF.Relu, [pk(bank)], [K(r_)])
                        tt(hid[:, hc, half * 512:(half + 1) * 512], r_[:], r_[:], ALU.mult, [K(r_)], [K(hid, hc, half)])
            for cb in range(4):
                sl = next_slot()
                dma("pool", sl[:], w2_v[:, g, :, cb * 512:(cb + 1) * 512], writes=SK(sl))
                for t8 in range(8):
                    bank = 4 + it2 % 4
                    it2 += 1
                    for kc in range(16):
                        mm(pb[bank][:], hid[:, kc, t8 * 128:(t8 + 1) * 128], sl[:, kc, :], kc == 0, kc == 15,
                           SK(sl) + [K(hid, kc, t8 // 4)], [pk(bank)])
                    tw = tmpw[it2 % 2]
                    tt(tw[:], pb[bank][:], rowb[:, cb * 512:(cb + 1) * 512], ALU.mult, [pk(bank), K(rowb)], [K(tw)])
                    tt(acc[:, t8, cb * 512:(cb + 1) * 512], acc[:, t8, cb * 512:(cb + 1) * 512], tw[:], ALU.add,
                       [K(acc, t8, cb), K(tw)], [K(acc, t8, cb)])
        phase_end()
        cur["slots"].free()
        ar.release(hid, *rl, mg, *tmpw)
        dma("sp", rowb[:], dram["fgrow"], writes=[K(rowb)])
        junk = ar.alloc("junk3", [128, D], BF16)
        ss = ar.alloc("ss3", [128, 8], F32)
        for t8 in range(8):
            ak = [K(acc, t8, cb) for cb in range(4)]
            memset(ss[:, t8:t8 + 1], 0.0, [K(ss, t8)])
            act(junk[:], acc[:, t8, :], AF.Square, ak + [K(ss, t8)], [K(junk), K(ss, t8)], accum_out=ss[:, t8:t8 + 1])
            act(ss[:, t8:t8 + 1], ss[:, t8:t8 + 1], AF.Sqrt, [K(ss, t8), K(eps_t)], [K(ss, t8)], bias=eps_t[:], scale=1.0 / D)
            recip(ss[:, t8:t8 + 1], K(ss, t8))
            stt(acc[:, t8, :], acc[:, t8, :], ss[:, t8:t8 + 1], rowb[:], ALU.mult, ALU.mult, ak + [K(ss, t8), K(rowb)], ak)
            out_ops.append(dma("sp", out_d[base + t8 * 128:base + (t8 + 1) * 128, :], acc[:, t8, :], reads=ak))
        phase_end()
        ar.release(junk, ss, acc, rowb)

    return finalize()


def _t5_buckets():
    qi = np.arange(128)[:, None]
    ki = np.arange(256)[None, :]
    n = np.maximum(qi + 128 - ki, 0)
    max_exact = 16
    large = max_exact + (np.log(np.maximum(n, 1) / max_exact) / np.log(128 / max_exact) * (32 - max_exact)).astype(np.int32)
    large = np.minimum(large, 31)
    return np.where(n < max_exact, n, large).astype(np.int32)


def _fm(v, nchunk):
    return np.ascontiguousarray(np.asarray(v, np.float32).reshape(nchunk, 128).T)


def make_in_maps(inp):
    f32 = np.float32
    x = np.asarray(inp["x"], f32)
    c = np.asarray(inp["c"], f32)
    b_in = np.asarray(inp["b_in"], f32)[0]
    shared = {}
    shared["w_ada"] = np.ascontiguousarray(np.asarray(inp["w_ada"], f32)[0])
    shared["b_ada"] = _fm(np.asarray(inp["b_ada"], f32)[0], 96)
    shared["n1g"] = _fm(np.asarray(inp["norm1_g"], f32)[0], 16)
    shared["n2g"] = _fm(np.asarray(inp["norm2_g"], f32)[0], 16)
    shared["fgrow"] = np.ascontiguousarray(np.broadcast_to(np.asarray(inp["final_g"], f32)[None, :], (128, D)))
    shared["w_in"] = np.ascontiguousarray(np.asarray(inp["w_in"], f32)[0])
    shared["b_q"] = _fm(b_in[0:1024], 8)
    bk = np.zeros((128, 8), f32)
    for g in range(4):
        for e in range(2):
            bk[e * 64:(e + 1) * 64, g * 2 + e] = b_in[1024 + g * 64:1024 + (g + 1) * 64]
    shared["b_k"] = bk
    bv = np.zeros((512,), f32)
    for g in range(4):
        for e in range(2):
            bv[g * 128 + e * 64:g * 128 + (e + 1) * 64] = b_in[1280 + g * 64:1280 + (g + 1) * 64]
    shared["b_vrow"] = np.ascontiguousarray(np.broadcast_to(bv[None, :], (128, 512)))
    shared["b_u"] = _fm(b_in[1536:2048], 4)
    shared["b_ga"] = _fm(b_in[2048:4096], 16)
    shared["b_gs"] = _fm(b_in[4096:6144], 16)
    shared["sinks"] = np.ascontiguousarray(np.broadcast_to(np.asarray(inp["attn_sinks"], f32)[0][None, :], (128, 16)))
    bk_ = _t5_buckets()
    rb = np.asarray(inp["rel_bias"], f32)
    shared["biasg"] = np.ascontiguousarray(np.transpose(rb[bk_], (0, 2, 1)))
    qi = np.arange(128)[:, None]
    ki = np.arange(256)[None, :]
    dist = qi + 128 - ki
    band = (dist >= 0) & (dist < 128)
    shared["bandm"] = np.where(band, 0.0, NEG).astype(f32)
    rep2 = lambda a: np.ascontiguousarray(np.concatenate([a, a], axis=0))
    shared["lre"] = rep2(np.asarray(inp["lambda_re"], f32)[0].T)
    shared["lim"] = rep2(np.asarray(inp["lambda_im"], f32)[0].T)
    shared["lst"] = np.ascontiguousarray(np.broadcast_to(np.asarray(inp["log_step"], f32)[0][None, :], (128, 32)))
    shared["bre"] = rep2(np.transpose(np.asarray(inp["ssm_b_re"], f32)[0], (1, 0, 2)))
    shared["bim"] = rep2(np.transpose(np.asarray(inp["ssm_b_im"], f32)[0], (1, 0, 2)))
    shared["cre"] = rep2(np.transpose(np.asarray(inp["ssm_c_re"], f32)[0], (2, 0, 1)))
    shared["cim"] = rep2(np.transpose(np.asarray(inp["ssm_c_im"], f32)[0], (2, 0, 1)))
    shared["dsk"] = _fm(np.asarray(inp["ssm_d"], f32)[0], 4)
    p = np.arange(128)
    shared["maskE"] = np.stack([((p // 16) % 2 == e) for e in range(2)], 1).astype(f32)
    shared["maskH"] = np.stack([(p // 64 == e) for e in range(2)], 1).astype(f32)
    shared["maskP"] = np.stack([(p // 32 == r) for r in range(4)], 1).astype(f32)
    shared["bdmask"] = (p[:, None] // 16 == p[None, :] // 16).astype(f32)
    shared["identf"] = np.eye(128, dtype=f32)
    shared["w_glu"] = np.ascontiguousarray(np.asarray(inp["w_glu"], f32)[0])
    shared["b_glu"] = _fm(np.asarray(inp["b_glu"], f32)[0], 4)
    shared["w_ap"] = np.ascontiguousarray(np.asarray(inp["w_attn_proj"], f32)[0])
    shared["w_sp"] = np.ascontiguousarray(np.asarray(inp["w_ssm_proj"], f32)[0])
    shared["w_out"] = np.ascontiguousarray(np.asarray(inp["w_out"], f32)[0])
    shared["w_ff1"] = np.ascontiguousarray(np.asarray(inp["w_ff1"], f32)[0])
    shared["w_ff2"] = np.ascontiguousarray(np.asarray(inp["w_ff2"], f32)[0])
    maps = []
    zeros_x = np.zeros((TOK, D), f32)
    am0 = np.zeros((128, 256), f32)
    am1 = np.zeros((128, 256), f32)
    am1[:, 0:128] = NEG
    for core in range(NCORES):
        b, hf = core // 2, core % 2
        m = dict(shared)
        m["x"] = np.ascontiguousarray(x[b, hf * TOK:(hf + 1) * TOK])
        m["xp"] = np.ascontiguousarray(x[b, 0:TOK]) if hf == 1 else zeros_x
        m["cvec"] = _fm(c[b], 16)
        m["flag"] = np.full((128, 1), float(hf), f32)
        m["amask"] = am0 if hf == 1 else am1
        maps.append(m)
    return maps


_CACHE = {}


def kernel(**inputs):
    if "nc" not in _CACHE:
        _CACHE["nc"] = build_program()[0]
    nc = _CACHE["nc"]
    maps = make_in_maps(inputs)
    res = run_bass_kernel_spmd(nc, maps, core_ids=list(range(NCORES)))
    out = np.empty((4, 4096, D), np.float32)
    for core in range(NCORES):
        b, hf = core // 2, core % 2
        out[b, hf * TOK:(hf + 1) * TOK] = np.asarray(res.results[core]["out"], np.float32)
    return out
```
